# Optimizing a Trainium2 kernel written in Bass

```python
import jax, jax.numpy as jnp
from jax import lax
import numpy as np

D_MODEL = 1024
BATCH = 32
SEQ = 2048
DEPTH = 4
DEC_BATCH = 16
DEC_SEQ = 16
PAST_LEN = 2048

CHUNK = 64
Q_BLOCK = 128
N_HEADS = 8
QK_NOPE_DIM = 64
QK_ROPE_DIM = 32
V_HEAD_DIM = 64
Q_LORA = 384
KV_LORA = 256
ROPE_THETA = 10000.0
ATTN_SCALE = (QK_NOPE_DIM + QK_ROPE_DIM) ** -0.5
CONF_CH = 256
CONF_WIDTH = 31
SC_CH = 256
SC_WIDTH = 3
N_BRANCH = 3
D_FF = 4 * D_MODEL
DN_ALPHA = (2 * DEPTH) ** 0.25
DN_BETA = (8 * DEPTH) ** -0.25
LN_EPS = 1e-5
RMS_EPS = 1e-6
SPLITS = (Q_LORA, KV_LORA, QK_ROPE_DIM, 2 * CONF_CH, 3 * SC_CH, N_BRANCH * D_MODEL)
N_IN = Q_LORA + KV_LORA + QK_ROPE_DIM + 2 * CONF_CH + 3 * SC_CH + N_BRANCH * D_MODEL

kernel_name = "mla_conformer_shortconv_deepnorm_step"


def layer_norm(x, g, b):
    xf = x.astype(jnp.float32)
    mu = xf.mean(-1, keepdims=True)
    var = jnp.square(xf - mu).mean(-1, keepdims=True)
    return ((xf - mu) * lax.rsqrt(var + LN_EPS) * g.astype(jnp.float32) + b.astype(jnp.float32)).astype(x.dtype)


def rms_norm(x, g):
    xf = x.astype(jnp.float32)
    ms = jnp.square(xf).mean(-1, keepdims=True)
    return (xf * lax.rsqrt(ms + RMS_EPS) * g.astype(jnp.float32)).astype(x.dtype)


def rope(x, pos):
    half = QK_ROPE_DIM // 2
    inv = ROPE_THETA ** (-jnp.arange(half, dtype=jnp.float32) / half)
    ang = pos.astype(jnp.float32)[:, None] * inv[None, :]
    cos = jnp.cos(ang)[None, :, None, :]
    sin = jnp.sin(ang)[None, :, None, :]
    xf = x.astype(jnp.float32)
    x1, x2 = xf[..., :half], xf[..., half:]
    return jnp.concatenate([x1 * cos - x2 * sin, x1 * sin + x2 * cos], axis=-1).astype(x.dtype)


def depthwise_conv(xp, w):
    c = xp.shape[-1]
    return lax.conv_general_dilated(xp, w[:, None, :].astype(xp.dtype), window_strides=(1,), padding='VALID',
                                    dimension_numbers=('NWC', 'WIO', 'NWC'), feature_group_count=c)


def mla_block(q_nope, q_rope, q_pos, ckv, krope, k_pos, w_uk, w_uv):
    q_lat = jnp.einsum('bqhd,chd->bqhc', q_nope, w_uk)
    s = jnp.einsum('bqhc,bkc->bhqk', q_lat, ckv) + jnp.einsum('bqhr,bkr->bhqk', q_rope, krope)
    s = s.astype(jnp.float32) * ATTN_SCALE
    mask = (k_pos[None, :] // CHUNK) <= (q_pos[:, None] // CHUNK)
    s = jnp.where(mask[None, None], s, -1e30)
    p = jax.nn.softmax(s, axis=-1).astype(ckv.dtype)
    o_lat = jnp.einsum('bhqk,bkc->bqhc', p, ckv)
    return jnp.einsum('bqhc,chd->bqhd', o_lat, w_uv)


def mla_attention(q_nope, q_rope, q_pos, ckv, krope, k_pos, w_uk, w_uv):
    b, l = q_nope.shape[:2]
    if l > Q_BLOCK and l % Q_BLOCK == 0:
        nb = l // Q_BLOCK

        def blk(a):
            return a.reshape((b, nb, Q_BLOCK) + a.shape[2:]).swapaxes(0, 1)

        out = lax.map(lambda t: mla_block(t[0], t[1], t[2], ckv, krope, k_pos, w_uk, w_uv),
                      (blk(q_nope), blk(q_rope), q_pos.reshape(nb, Q_BLOCK)))
        return out.swapaxes(0, 1).reshape(b, l, N_HEADS, V_HEAD_DIM)
    return mla_block(q_nope, q_rope, q_pos, ckv, krope, k_pos, w_uk, w_uv)


def trunk_layer(x, past_ckv, past_krope, past_conf, past_sc, p):
    (w_in, b_gate, q_norm_g, w_uq, kv_norm_g, w_uk, w_uv, w_mla_out, conf_dw_w, conf_dw_b, conf_ln_g, conf_ln_b,
     w_conf_out, sc_dw_w, w_sc_out, w_mix_out, ln1_g, ln1_b, w_ff1, b_ff1, w_ff2, b_ff2, ln2_g, ln2_b) = p
    b, l, _ = x.shape
    past = past_ckv.shape[1]
    proj = x @ w_in
    offs = [int(o) for o in np.cumsum(SPLITS)[:-1]]
    q_c, kv_c, k_r, conf_in, sc_in, gate_in = jnp.split(proj, offs, axis=-1)
    pos = past + jnp.arange(l)

    q = jnp.einsum('blc,chd->blhd', rms_norm(q_c, q_norm_g), w_uq)
    q_nope = q[..., :QK_NOPE_DIM]
    q_rope = rope(q[..., QK_NOPE_DIM:], pos)
    c_kv = rms_norm(kv_c, kv_norm_g)
    k_rope = rope(k_r[:, :, None, :], pos)[:, :, 0]
    keys_ckv = jnp.concatenate([past_ckv, c_kv], axis=1)
    keys_kr = jnp.concatenate([past_krope, k_rope], axis=1)
    k_pos = jnp.arange(past + l)
    o = mla_attention(q_nope, q_rope, pos, keys_ckv, keys_kr, k_pos, w_uk, w_uv)
    mla_out = o.reshape(b, l, N_HEADS * V_HEAD_DIM) @ w_mla_out

    ca, cg = jnp.split(conf_in, 2, axis=-1)
    u = ca * jax.nn.sigmoid(cg)
    up = jnp.concatenate([past_conf, u], axis=1)
    new_conf = up[:, -(CONF_WIDTH - 1):]
    cv = depthwise_conv(up, conf_dw_w) + conf_dw_b
    conf_out = jax.nn.silu(layer_norm(cv, conf_ln_g, conf_ln_b)) @ w_conf_out

    gb, gc, h = jnp.split(sc_in, 3, axis=-1)
    z = gc * h
    zp = jnp.concatenate([past_sc, z], axis=1)
    new_sc = zp[:, -(SC_WIDTH - 1):]
    sc_out = (gb * depthwise_conv(zp, sc_dw_w)) @ w_sc_out

    g = jax.nn.sigmoid(gate_in.reshape(b, l, N_BRANCH, D_MODEL) + b_gate)
    merged = g[:, :, 0] * mla_out + g[:, :, 1] * conf_out + g[:, :, 2] * sc_out
    x = layer_norm(DN_ALPHA * x + merged @ w_mix_out, ln1_g, ln1_b)

    hdn = jnp.square(jax.nn.relu(x @ w_ff1 + b_ff1))
    x = layer_norm(DN_ALPHA * x + (hdn @ w_ff2 + b_ff2), ln2_g, ln2_b)
    return x, c_kv, k_rope, new_conf, new_sc


def setup_inputs(seed: int = 0) -> dict:
    key = jax.random.key(seed)
    ks = jax.random.split(key, 32)
    f32 = jnp.float32

    def nrm(k, shape, scale):
        return jax.random.normal(k, shape, f32) * scale

    return {
        "x_prompt": nrm(ks[0], (BATCH, SEQ, D_MODEL), 1.0),
        "x_sample": nrm(ks[1], (DEC_BATCH, DEC_SEQ, D_MODEL), 1.0),
        "cache_ckv": nrm(ks[2], (DEPTH, DEC_BATCH, PAST_LEN, KV_LORA), 1.0),
        "cache_krope": nrm(ks[3], (DEPTH, DEC_BATCH, PAST_LEN, QK_ROPE_DIM), 1.0),
        "state_conf": nrm(ks[4], (DEPTH, DEC_BATCH, CONF_WIDTH - 1, CONF_CH), 0.5),
        "state_sc": nrm(ks[5], (DEPTH, DEC_BATCH, SC_WIDTH - 1, SC_CH), 0.5),
        "w_in": nrm(ks[6], (DEPTH, D_MODEL, N_IN), D_MODEL ** -0.5),
        "b_gate": nrm(ks[7], (DEPTH, N_BRANCH, D_MODEL), 0.02),
        "q_norm_g": 1.0 + nrm(ks[8], (DEPTH, Q_LORA), 0.02),
        "w_uq": nrm(ks[9], (DEPTH, Q_LORA, N_HEADS, QK_NOPE_DIM + QK_ROPE_DIM), Q_LORA ** -0.5),
        "kv_norm_g": 1.0 + nrm(ks[10], (DEPTH, KV_LORA), 0.02),
        "w_uk": nrm(ks[11], (DEPTH, KV_LORA, N_HEADS, QK_NOPE_DIM), KV_LORA ** -0.5),
        "w_uv": nrm(ks[12], (DEPTH, KV_LORA, N_HEADS, V_HEAD_DIM), DN_BETA * KV_LORA ** -0.5),
        "w_mla_out": nrm(ks[13], (DEPTH, N_HEADS * V_HEAD_DIM, D_MODEL), (N_HEADS * V_HEAD_DIM) ** -0.5),
        "conf_dw_w": nrm(ks[14], (DEPTH, CONF_WIDTH, CONF_CH), CONF_WIDTH ** -0.5),
        "conf_dw_b": nrm(ks[15], (DEPTH, CONF_CH), 0.02),
        "conf_ln_g": 1.0 + nrm(ks[16], (DEPTH, CONF_CH), 0.02),
        "conf_ln_b": nrm(ks[17], (DEPTH, CONF_CH), 0.02),
        "w_conf_out": nrm(ks[18], (DEPTH, CONF_CH, D_MODEL), CONF_CH ** -0.5),
        "sc_dw_w": nrm(ks[19], (DEPTH, SC_WIDTH, SC_CH), SC_WIDTH ** -0.5),
        "w_sc_out": nrm(ks[20], (DEPTH, SC_CH, D_MODEL), SC_CH ** -0.5),
        "w_mix_out": nrm(ks[21], (DEPTH, D_MODEL, D_MODEL), DN_BETA * D_MODEL ** -0.5),
        "ln1_g": 1.0 + nrm(ks[22], (DEPTH, D_MODEL), 0.02),
        "ln1_b": nrm(ks[23], (DEPTH, D_MODEL), 0.02),
        "w_ff1": nrm(ks[24], (DEPTH, D_MODEL, D_FF), DN_BETA * D_MODEL ** -0.5),
        "b_ff1": nrm(ks[25], (DEPTH, D_FF), 0.02),
        "w_ff2": nrm(ks[26], (DEPTH, D_FF, D_MODEL), DN_BETA * D_FF ** -0.5),
        "b_ff2": nrm(ks[27], (DEPTH, D_MODEL), 0.02),
        "ln2_g": 1.0 + nrm(ks[28], (DEPTH, D_MODEL), 0.02),
        "ln2_b": nrm(ks[29], (DEPTH, D_MODEL), 0.02),
    }


def reference(x_prompt, x_sample, cache_ckv, cache_krope, state_conf, state_sc, w_in, b_gate, q_norm_g, w_uq,
              kv_norm_g, w_uk, w_uv, w_mla_out, conf_dw_w, conf_dw_b, conf_ln_g, conf_ln_b, w_conf_out, sc_dw_w,
              w_sc_out, w_mix_out, ln1_g, ln1_b, w_ff1, b_ff1, w_ff2, b_ff2, ln2_g, ln2_b):
    def layer_params(i):
        return (w_in[i], b_gate[i], q_norm_g[i], w_uq[i], kv_norm_g[i], w_uk[i], w_uv[i], w_mla_out[i],
                conf_dw_w[i], conf_dw_b[i], conf_ln_g[i], conf_ln_b[i], w_conf_out[i], sc_dw_w[i], w_sc_out[i],
                w_mix_out[i], ln1_g[i], ln1_b[i], w_ff1[i], b_ff1[i], w_ff2[i], b_ff2[i], ln2_g[i], ln2_b[i])

    def run(x, past_ckv, past_krope, past_conf, past_sc):
        ckvs, krs, confs, scs = [], [], [], []
        for i in range(DEPTH):
            x, c_kv, k_rope, n_conf, n_sc = trunk_layer(x, past_ckv[i], past_krope[i], past_conf[i], past_sc[i],
                                                         layer_params(i))
            ckvs.append(c_kv)
            krs.append(k_rope)
            confs.append(n_conf)
            scs.append(n_sc)
        return x, jnp.stack(ckvs), jnp.stack(krs), jnp.stack(confs), jnp.stack(scs)

    bp = x_prompt.shape[0]
    dt = x_prompt.dtype
    y_prompt, ckv_p, kr_p, conf_p, sc_p = run(
        x_prompt,
        jnp.zeros((DEPTH, bp, 0, KV_LORA), dt),
        jnp.zeros((DEPTH, bp, 0, QK_ROPE_DIM), dt),
        jnp.zeros((DEPTH, bp, CONF_WIDTH - 1, CONF_CH), dt),
        jnp.zeros((DEPTH, bp, SC_WIDTH - 1, SC_CH), dt))
    y_sample, ckv_s, kr_s, conf_s, sc_s = run(x_sample, cache_ckv, cache_krope, state_conf, state_sc)
    return (y_prompt, y_sample, ckv_p, kr_p, conf_p, sc_p, ckv_s, kr_s, conf_s, sc_s)
```

```python
import numpy as np
import concourse.bass as bass
import concourse.mybir as mybir
from concourse.bass_utils import run_bass_kernel_spmd

F32 = mybir.dt.float32
BF16 = mybir.dt.bfloat16
AF = mybir.ActivationFunctionType
ALU = mybir.AluOpType
ESZ = {F32: 4, BF16: 2}


class V:
    __slots__ = ("ap", "key", "lo", "hi")

    def __init__(self, ap, key, lo, hi):
        self.ap, self.key, self.lo, self.hi = ap, key, lo, hi


class Buf:
    def __init__(self, ap, key, shape, dtype, off=0):
        self.ap, self.key, self.shape, self.dtype, self.off = ap, key, list(shape), dtype, off
        self.es = ESZ[dtype]
        st = [1] * len(shape)
        for i in range(len(shape) - 2, 0, -1):
            st[i] = st[i + 1] * shape[i + 1]
        self.st = st

    def __getitem__(self, idx):
        if not isinstance(idx, tuple):
            idx = (idx,)
        idx = list(idx) + [slice(None)] * (len(self.shape) - len(idx))
        lo = 0
        hi = 0
        for d in range(1, len(self.shape)):
            i = idx[d]
            if isinstance(i, slice):
                a = 0 if i.start is None else i.start
                b = self.shape[d] if i.stop is None else i.stop
            else:
                a, b = i, i + 1
            assert 0 <= a < b <= self.shape[d], (self.key, idx, self.shape)
            lo += a * self.st[d]
            hi += (b - 1) * self.st[d]
        if getattr(self, "whole", False):
            return V(self.ap[tuple(idx)], self.key, 0, 1 << 30)
        return V(self.ap[tuple(idx)], self.key, self.off + lo * self.es, self.off + (hi + 1) * self.es)


class Op:
    __slots__ = ("eng", "fn", "deps", "signal", "stream", "cnt", "idx")


class Kern:
    ENGS = ("pe", "act", "dve", "pool", "sp")

    def __init__(self, nc):
        self.nc = nc
        self.ops = {e: [] for e in self.ENGS}
        self.rec = {}
        self.streams = {}
        self.ctx = []

    def sbuf(self, name, shape, dtype):
        cm = self.nc.sbuf_tensor(name, list(shape), dtype)
        t = cm.__enter__()
        self.ctx.append(cm)
        b = Buf(t[tuple(slice(None) for _ in shape)], name, shape, dtype)
        b.t = t
        return b

    def psum(self, name, shape, dtype=F32):
        cm = self.nc.psum_tensor(name, list(shape), dtype)
        t = cm.__enter__()
        self.ctx.append(cm)
        b = Buf(t[tuple(slice(None) for _ in shape)], name, shape, dtype)
        b.t = t
        b.whole = True
        return b

    def sub(self, parent, name, off_bytes, shape, dtype):
        n = int(np.prod(shape[1:]))
        es_p = parent.es
        nb = n * ESZ[dtype]
        assert off_bytes % 4 == 0 and off_bytes + nb <= parent.shape[1] * es_p, (name, off_bytes, nb)
        ap = parent.ap[0:shape[0], off_bytes // es_p:(off_bytes + nb) // es_p]
        if dtype != parent.dtype:
            ap = ap.bitcast(dtype)
        if len(shape) == 3:
            ap = ap.rearrange("p (a b) -> p a b", b=shape[2])
        return Buf(ap, parent.key, shape, dtype, off=parent.off + off_bytes)

    def add(self, eng, fn, reads=(), writes=(), stream=None):
        op = Op()
        op.eng, op.fn, op.signal, op.stream = eng, fn, False, stream
        lst = self.ops[eng]
        op.idx = len(lst)
        deps = set()
        for v in reads:
            for r in self.rec.get(v.key, ()):
                if r[2] == "W" and r[0] < v.hi and v.lo < r[1]:
                    deps.add(r[3])
        for v in writes:
            for r in self.rec.get(v.key, ()):
                if r[0] < v.hi and v.lo < r[1]:
                    deps.add(r[3] if r[2] == "W" else ("war",) + r[3])
        if stream is not None:
            self.streams[stream] = self.streams.get(stream, 0) + 1
            op.cnt = self.streams[stream]
            who = ("dma", stream, op.cnt)
        else:
            who = ("eng", eng, op.idx)
        fdeps = set()
        for d in deps:
            war = d[0] == "war"
            if war:
                d = d[1:]
            if d[0] == "dma" and d[1].startswith("cast"):
                d = ("dma", d[1], self.streams[d[1]])
            if d[0] == "eng" and d[1] == eng and stream is None:
                if eng == "pe":
                    continue
            fdeps.add(d)
        op.deps = fdeps
        for v in reads:
            L = self.rec.setdefault(v.key, [])
            for r in L:
                if r[2] == "R" and r[0] == v.lo and r[1] == v.hi and r[3][0] == who[0] and r[3][1] == who[1]:
                    r[3] = who
                    break
            else:
                L.append([v.lo, v.hi, "R", who])
        for v in writes:
            L = self.rec.setdefault(v.key, [])
            L[:] = [r for r in L if not (v.lo <= r[0] and r[1] <= v.hi)]
            L.append([v.lo, v.hi, "W", who])
        lst.append(op)
        return op

    def dma(self, q, stream, out, in_, out_v=None, in_v=None):
        reads = [in_v] if in_v is not None else []
        writes = [out_v] if out_v is not None else []
        return self.add(q, lambda e: e.dma_start(out=out, in_=in_), reads, writes, stream=stream)

    def emit(self):
        nc = self.nc
        for e in self.ENGS:
            for op in self.ops[e]:
                for d in op.deps:
                    if d[0] == "eng":
                        self.ops[d[1]][d[2]].signal = True
        sigcnt = {}
        for e in self.ENGS:
            c = 0
            arr = []
            for op in self.ops[e]:
                if op.signal and op.stream is None:
                    c += 1
                arr.append(c)
            sigcnt[e] = arr
        sems = {}
        for e in self.ENGS:
            cm = nc.semaphore("s_" + e)
            sems[("eng", e)] = cm.__enter__()
            self.ctx.append(cm)
        for s in self.streams:
            cm = nc.semaphore("d_" + s)
            sems[("dma", s)] = cm.__enter__()
            self.ctx.append(cm)
        total_waits = [0]

        def emit_eng(e, eng):
            waited = {}
            for op in self.ops[e]:
                need = {}
                for d in op.deps:
                    if d[0] == "eng":
                        k = ("eng", d[1])
                        val = sigcnt[d[1]][d[2]]
                    else:
                        k = ("dma", d[1])
                        val = 16 * d[2]
                    if val > need.get(k, 0):
                        need[k] = val
                for k, val in need.items():
                    if val > waited.get(k, 0):
                        eng.wait_ge(sems[k], val)
                        waited[k] = val
                        total_waits[0] += 1
                ins = op.fn(eng)
                if op.stream is not None:
                    ins.then_inc(sems[("dma", op.stream)], 16)
                elif op.signal:
                    ins.then_inc(sems[("eng", e)], 1)
            if e == "sp":
                for s, c in self.streams.items():
                    eng.wait_ge(sems[("dma", s)], 16 * c)

        with nc.Block() as block:
            @block.tensor
            def _(eng):
                emit_eng("pe", eng)

            @block.scalar
            def _(eng):
                emit_eng("act", eng)

            @block.vector
            def _(eng):
                emit_eng("dve", eng)

            @block.gpsimd
            def _(eng):
                emit_eng("pool", eng)

            @block.sync
            def _(eng):
                emit_eng("sp", eng)
        self.total_waits = total_waits[0]

    def close(self):
        for cm in reversed(self.ctx):
            cm.__exit__(None, None, None)
        self.ctx = []

    def mm(self, out, lhsT, rhs, start=True, stop=True):
        return self.add("pe", lambda e: e.matmul(out.ap, lhsT.ap, rhs.ap, start=start, stop=stop),
                        [lhsT, rhs], [out])

    def tr(self, out, in_, ident):
        return self.add("pe", lambda e: e.transpose(out.ap, in_.ap, ident.ap), [in_, ident], [out])

    def act(self, out, in_, func, bias=None, scale=None, eng="act"):
        kw = {}
        rd = [in_]
        if bias is not None:
            if isinstance(bias, V):
                kw["bias"] = bias.ap
                rd.append(bias)
            else:
                kw["bias"] = bias
        if scale is not None:
            if isinstance(scale, V):
                kw["scale"] = scale.ap
                rd.append(scale)
            else:
                kw["scale"] = scale
        return self.add("act", lambda e: e.activation(out.ap, in_.ap, func, **kw), rd, [out])

    def tt(self, eng, out, in0, in1, op):
        return self.add(eng, lambda e: e.tensor_tensor(out.ap, in0.ap, in1.ap, op), [in0, in1], [out])

    def ts(self, eng, out, in0, s1, op0, s2=None, op1=None):
        rd = [in0]
        a1 = s1
        a2 = s2
        if isinstance(s1, V):
            rd.append(s1)
            a1 = s1.ap
        if isinstance(s2, V):
            rd.append(s2)
            a2 = s2.ap
        if op1 is None:
            return self.add(eng, lambda e: e.tensor_scalar(out.ap, in0.ap, a1, None, op0), rd, [out])
        return self.add(eng, lambda e: e.tensor_scalar(out.ap, in0.ap, a1, a2, op0, op1), rd, [out])

    def stt(self, out, in0, s, in1, op0, op1):
        rd = [in0, in1]
        a = s
        if isinstance(s, V):
            rd.append(s)
            a = s.ap
        return self.add("dve", lambda e: e.scalar_tensor_tensor(out.ap, in0.ap, a, in1.ap, op0, op1), rd, [out])

    def copy(self, eng, out, in_):
        if eng == "act":
            return self.act(out, in_, AF.Copy)
        return self.add(eng, lambda e: e.tensor_copy(out.ap, in_.ap), [in_], [out])

    def memset(self, eng, out, val):
        return self.add(eng, lambda e: e.memset(out.ap, val), [], [out])

NL = 4
NSP = 4
NSS = 2
SEQ = 2048
TS = 16
PAST = 2048
DM = 1024
NPOS = PAST + TS
NBLK = 35
BW = 4096
NCOL = 135
C_BG, C_QG, C_KVG, C_CDB, C_CLG, C_CLB, C_L1G, C_L1B, C_BF1, C_BF2, C_L2G, C_L2B = (
    0, 24, 27, 29, 31, 33, 35, 43, 51, 83, 91, 99)
C_HBG, C_HCLG, C_HCLB = 107, 131, 133
B_AKV, B_KV, B_AQ, B_UQ, B_C, B_CD0, B_CD1, B_S0, B_S1, B_M0, B_X0, B_X1, B_F1A, B_F2A, B_F1B, B_F2B = (
    0, 1, 2, 3, 4, 5, 6, 7, 8, 9, 17, 18, 19, 23, 27, 31)
BUSED = [3072, 2048, 3072, 3072, 4096, 3968, 3968, 4096, 2816] + [4096] * 26
ATTN_SCALE = 96.0 ** -0.5
DN_ALPHA = 8.0 ** 0.25
LN_EPS = 1e-5
RMS_EPS = 1e-6
NWS = 4


def build_program(nsp=NSP, nss=NSS, nl=NL):
    import os
    KSTOP = int(os.environ.get('KSTOP', '99'))
    KTILES = int(os.environ.get('KTILES', '4'))
    KSUB = int(os.environ.get('KSUB', '99'))
    KING = int(os.environ.get('KING', '99'))
    nc = bass.Bass("TRN2", target_bir_lowering=False)
    K = Kern(nc)

    def din(name, shape, dt=F32):
        return nc.dram_tensor(name, list(shape), dt, kind="ExternalInput").ap()

    def dout(name, shape):
        return nc.dram_tensor(name, list(shape), F32, kind="ExternalOutput").ap()

    xp = din("xp", [NSP, SEQ, DM])
    xs = din("xs", [NSS, TS, DM])
    cckv = din("cckv", [NL, NSS, PAST, 256])
    ckr = din("ckr", [NL, NSS, PAST, 32])
    sconf = din("sconf", [NL, NSS, 30, 256])
    ssc = din("ssc", [NL, NSS, 2, 256])
    wf32 = din("wf32", [NL * NBLK * 128, BW])
    cvec = din("cvec", [128, NL * NCOL])
    rope = din("rope", [128, NPOS])
    identd = din("identd", [128, 128])
    wbf = nc.dram_tensor("wbf", [NL * NBLK * 128, BW], BF16, kind="Internal").ap()
    yp = dout("yp", [NSP, SEQ, DM])
    ys = dout("ys", [NSS, TS, DM])
    ockvp = dout("ockvp", [NL, NSP, SEQ, 256])
    okrp = dout("okrp", [NL, NSP, SEQ, 32])
    oconfp = dout("oconfp", [NL, NSP, 30, 256])
    oscp = dout("oscp", [NL, NSP, 2, 256])
    ockvs = dout("ockvs", [NL, NSS, TS, 256])
    okrs = dout("okrs", [NL, NSS, TS, 32])
    oconfs = dout("oconfs", [NL, NSS, 30, 256])
    oscs = dout("oscs", [NL, NSS, 2, 256])

    xres = K.sbuf("xres", [128, 4, 8, 512], F32)
    KT = K.sbuf("KT", [128, 8, NPOS], BF16)
    Vb = K.sbuf("Vb", [128, 17, 8, 65], BF16)
    wbuf = K.sbuf("wbuf", [128, NWS, BW], BF16)
    xbf = K.sbuf("xbf", [128, 8, 512], BF16)
    upf = K.sbuf("upf", [128, 2, 544], F32)
    zpf = K.sbuf("zpf", [128, 2, 516], F32)
    cv = K.sbuf("cv", [128, NL * NCOL], F32)
    ropeT = K.sbuf("ropeT", [128, 2, 512], F32)
    ident = K.sbuf("ident", [128, 128], F32)
    ones = K.sbuf("ones", [128, 128], BF16)
    mhalf = K.sbuf("mhalf", [128, 512], F32)
    mone = K.sbuf("mone", [128, 512], F32)
    arena = K.sbuf("arena", [128, 16384], BF16)

    def sub(name, off, shape, dt):
        return K.sub(arena, name, off, shape, dt)

    st1 = sub("st1", 0, [128, 512], F32)
    st2 = sub("st2", 2048, [128, 512], F32)
    st3 = sub("st3", 4096, [128, 512], F32)
    tAB = sub("tAB", 6144, [128, 2, 512], F32)
    bfA = sub("bfA", 10240, [128, 4, 512], BF16)
    ckv_f = sub("ckv_f", 14336, [128, 2, 512], F32)
    ckv_bf = sub("ckv_bf", 18432, [128, 2, 512], BF16)
    kr_f = sub("kr_f", 20480, [128, 512], F32)
    outst = sub("outst", 22528, [128, 4, 256], F32)
    krst = sub("krst", 26624, [128, 4, 32], F32)
    qn_bf = sub("qn_bf", 14336, [128, 3, 512], BF16)
    QT = sub("QT", 17408, [128, 2, 512], BF16)
    PT = sub("PT", 19456, [128, 3, 512], BF16)
    recb = sub("recb", 22528, [128, 512], F32)
    d32 = sub("d32", 24576, [128, 512], F32)
    dhl = sub("dhl", 26624, [128, 2, 512], BF16)
    oT = sub("oT", 28672, [128, 4, 512], BF16)
    conf_act = sub("conf_act", 26624, [128, 2, 512], BF16)
    scin = sub("scin", 24576, [128, 2, 512], BF16)
    cvs = sub("cvs", 14336, [128, 2, 512], F32)
    up_bf = sub("up_bf", 18432, [128, 2, 544], BF16)
    h_sb = sub("h_sb", 20608, [128, 2, 512], F32)
    zp_bf = sub("zp_bf", 18432, [128, 2, 516], BF16)
    gb_sb = sub("gb_sb", 6144, [128, 2, 512], F32)
    mtmp = sub("mtmp", 0, [128, 2, 512], F32)
    macc = sub("macc", 4096, [128, 2, 512], F32)
    t_sb = sub("t_sb", 8192, [128, 2, 512], F32)
    merged_bf = sub("merged_bf", 16384, [128, 8, 512], BF16)
    hbuf = sub("hbuf", 14336, [128, 16, 512], BF16)
    xst = sub("xst", 14336, [128, 4, 1024], F32)
    cst = sub("cst", 14336, [128, 16, 256], F32)
    kst = sub("kst", 0, [128, 16, 96], F32)
    sst = sub("sst", 8192, [128, 256], F32)

    PS = [K.psum("ps%d" % i, [128, 512]) for i in range(8)]
    psrot = [0]

    def pb(pool=(0, 1, 2, 3)):
        psrot[0] += 1
        return PS[pool[psrot[0] % len(pool)]]

    identb = K.sbuf("identb", [128, 128], BF16)
    PSb = []
    for i in range(8):
        bb = Buf(PS[i].t[:, :].bitcast(BF16), PS[i].key, [128, 1024], BF16)
        bb.whole = True
        PSb.append(bb)

    def planes(v):
        a = v.ap.bitcast(BF16).rearrange("p (n t) -> p n t", t=2)
        return (V(a[:, :, 0], v.key, v.lo, v.hi), V(a[:, :, 1], v.key, v.lo, v.hi))

    def xpose(bank, srcs, Kp, M, dst, kbase=0):
        Pb = PSb[bank]
        off = 0
        idv = identb[kbase:kbase + Kp, kbase:kbase + Kp]
        for src in srcs:
            lo, hi = planes(src)
            K.tr(Pb[0:M, off:off + Kp], lo, idv)
            K.tr(Pb[0:M, 512 + off:512 + off + Kp], hi, idv)
            off += Kp
        dlo, dhi = planes(dst)
        K.copy("dve", dlo, Pb[0:M, 0:off])
        K.copy("dve", dhi, Pb[0:M, 512:512 + off])

    units = []
    for s in range(nsp):
        units.append(("p", s))
    for s in range(nss):
        units.append(("s", s))
    worder = []
    for (kind, s) in units:
        for l in range(nl):
            for i in range(4 if kind == "p" else 1):
                if kind == "s":
                    worder.append((l, B_KV))
                for b in range(NBLK):
                    worder.append((l, b))
    wstate = {"loaded": 0, "n": 0, "cast_next": 0, "owner": [-1] * NWS}
    cast_list = [(l, b) for l in range(nl) for b in range(NBLK)]

    def wreg(l, b):
        r = (l * NBLK + b)
        return V(None, "wbf", r * 16, r * 16 + 16)

    def do_cast(upto):
        while wstate["cast_next"] < min(upto, len(cast_list)):
            l, b = cast_list[wstate["cast_next"]]
            r0 = (l * NBLK + b) * 128
            src = wf32[r0:r0 + 128, :]
            dst = wbf[r0:r0 + 128, :]
            K.add("pool", lambda e, dst=dst, src=src: e.dma_start(out=dst, in_=src), [], [wreg(l, b)],
                  stream="cast%d" % (wstate["cast_next"] % 16))
            wstate["cast_next"] += 1

    def wload(n):
        l, b = worder[n]
        assert l * NBLK + b < wstate["cast_next"], ("cast not recorded", l, b)
        slot = n % NWS
        used = BUSED[b]
        r0 = (l * NBLK + b) * 128
        src = wbf[r0:r0 + 128, 0:used]
        dv = wbuf[:, slot, 0:used]
        wstate["owner"][slot] = n
        K.add("sp", lambda e, o=dv.ap, i=src: e.dma_start(out=o, in_=i), [wreg(l, b)], [dv], stream="w%d" % slot)

    def wget():
        n = wstate["n"]
        while wstate["loaded"] < min(n + NWS - 1, len(worder)):
            wload(wstate["loaded"])
            wstate["loaded"] += 1
        wstate["n"] = n + 1
        return n

    def W(nblk, off, n, p0=0, p1=128):
        slot = nblk % NWS
        assert wstate["owner"][slot] == nblk, ("weight slot recycled", nblk, wstate["owner"])
        return wbuf[p0:p1, slot, off:off + n]

    K.dma("sp", "cvl", cv[:, :].ap, cvec, out_v=cv[:, :])
    K.dma("sp", "idl", ident[:, :].ap, identd, out_v=ident[:, :])
    K.copy("dve", identb[:, :], ident[:, :])
    K.memset("dve", ones[:, :], 1.0)
    K.memset("dve", Vb[:, :, :, 64:65], 1.0)
    K.memset("dve", mhalf[:, :], -0.5)
    K.memset("dve", mone[:, :], -1.0)
    for l in range(NL):
        o = l * NCOL
        K.ts("dve", cv[:, o + C_HBG:o + C_HBG + 24], cv[:, o + C_BG:o + C_BG + 24], 0.5, ALU.mult)
        K.ts("dve", cv[:, o + C_HCLG:o + C_HCLG + 4], cv[:, o + C_CLG:o + C_CLG + 4], 0.5, ALU.mult)

    def cvc(l, off):
        return cv[:, l * NCOL + off:l * NCOL + off + 1]

    do_cast(NBLK)

    rope_n = [0]
    out_eng = [0]

    def alt(engs=("dve", "pool")):
        out_eng[0] += 1
        return engs[out_eng[0] % len(engs)]

    def proj(P, slot, woff, ncols, cc, T, rhsbuf, nk, start=True, stop=True):
        for k in range(nk):
            K.mm(P[:, 0:T], W(slot, woff + k * ncols + cc * 128, 128), rhsbuf[:, k, 0:T],
                 start=(start and k == 0), stop=(stop and k == nk - 1))

    def rstd_from(Pst, T, inv_n, eps):
        K.ts("dve", st1[:, 0:T], Pst[:, 0:T], inv_n, ALU.mult, eps, ALU.add)
        K.act(st1[:, 0:T], st1[:, 0:T], AF.Ln)
        K.act(st2[:, 0:T], st1[:, 0:T], AF.Exp, scale=-0.5)

    def ln_stats(P1, P2, T, n):
        K.ts("dve", st1[:, 0:T], P1[:, 0:T], 1.0 / n, ALU.mult)
        K.tt("dve", st3[:, 0:T], st1[:, 0:T], st1[:, 0:T], ALU.mult)
        K.stt(st3[:, 0:T], P2[:, 0:T], 1.0 / n, st3[:, 0:T], ALU.mult, ALU.subtract)
        K.ts("dve", st3[:, 0:T], st3[:, 0:T], LN_EPS, ALU.add)
        K.act(st3[:, 0:T], st3[:, 0:T], AF.Ln)
        K.act(st2[:, 0:T], st3[:, 0:T], AF.Exp, scale=-0.5)

    ckv_bf2 = sub("ckv_bf2", 6144, [128, 2, 512], BF16)

    def expand_kv(T, kcol0, kb0, slot, ckv_bf):
        for h in range(8):
            P = pb((0, 1, 2))
            for j in range(2):
                K.mm(P[0:64, 0:T], W(slot, j * 1024 + h * 64, 64), ckv_bf[:, j, 0:T], start=(j == 0), stop=(j == 1))
            if h % 2 == 0:
                K.copy("act", KT[0:64, h, kcol0:kcol0 + T], P[0:64, 0:T])
            else:
                K.copy("dve", KT[0:64, h, kcol0:kcol0 + T], P[0:64, 0:T])
        nsub = (T + 127) // 128
        for sb in range(nsub):
            nk = min(128, T - sb * 128)
            P = pb((0, 1, 2))
            for j in range(2):
                K.mm(P[0:nk, 0:512], ckv_bf[:, j, sb * 128:sb * 128 + nk], W(slot, j * 1024 + 512, 512),
                     start=(j == 0), stop=(j == 1))
            pv = P[0:nk, 0:512]
            pv3 = V(pv.ap.rearrange("p (h d) -> p h d", d=64), pv.key, pv.lo, pv.hi)
            K.copy("act" if sb % 2 == 0 else "dve", Vb[0:nk, kb0 + sb, :, 0:64], pv3)

    def layernorm_apply(xv, T, l, cg, cb, xbf_out):
        for c in range(8):
            K.tt("dve", xv(c), xv(c), st1[:, 0:T], ALU.subtract)
            K.tt("pool", xv(c), xv(c), st2[:, 0:T], ALU.mult)
            if xbf_out is not None:
                K.act(xbf_out[:, c, 0:T], xv(c), AF.Identity, bias=cvc(l, cb + c), scale=cvc(l, cg + c))
            K.act(xv(c), xv(c), AF.Identity, bias=cvc(l, cb + c), scale=cvc(l, cg + c))

    def tile(kind, s, l, i, first_block_cast, nl, pre_xbf=False, nxt=None):
        T = 512 if kind == "p" else TS
        pos0 = i * 512 if kind == "p" else PAST
        kb0 = pos0 // 128
        nsub = (T + 127) // 128
        ti = i if kind == "p" else 0

        def xv(c):
            return xres[:, ti, c, 0:T]

        bcount = [0]
        if kind == "s":
            ingest_cache(s, l)

        def nextw():
            if first_block_cast is not None and bcount[0] % 2 == 0:
                do_cast(wstate["cast_next"] + 1 if wstate["cast_next"] < first_block_cast else 0)
            bcount[0] += 1
            return wget()

        if l == 0:
            if kind == "p":
                src = xp[s, pos0:pos0 + 512, :].rearrange("(a p) d -> p a d", p=128)
                K.dma("sp", "xin", xst[:, :, :].ap, src, out_v=xst[:, :, :])
            else:
                K.dma("sp", "xin", xst[0:TS, 0, :].ap, xs[s, :, :], out_v=xst[0:TS, 0, :])
            nk0 = min(128, T)
            for c in range(8):
                xpose(c % 4, [xst[0:nk0, sb, c * 128:(c + 1) * 128] for sb in range(nsub)], nk0, 128, xv(c))
        if not pre_xbf:
            for c in range(8):
                K.copy(alt(), xbf[:, c, 0:T], xv(c))
        rs = rope_n[0] % 2
        rope_n[0] += 1
        K.dma("sp", "rope%d" % rs, ropeT[64:128, rs, 0:T].ap, rope[64:128, pos0:pos0 + T], out_v=ropeT[64:128, rs, 0:T])
        cosv = ropeT[64:96, rs, 0:T]
        sinv = ropeT[96:128, rs, 0:T]

        if KSTOP <= 0:
            return
        sAKV = nextw()
        Pkv = [PS[0], PS[1]]
        proj(Pkv[0], sAKV, 0, 384, 0, T, xbf, 8)
        proj(Pkv[1], sAKV, 0, 384, 1, T, xbf, 8)
        Pkr = PS[2]
        proj(Pkr, sAKV, 0, 384, 2, T, xbf, 8)
        for j in range(2):
            K.act(bfA[:, j, 0:T], Pkv[j][:, 0:T], AF.Square)
        Pst = PS[4]
        for j in range(2):
            K.mm(Pst[:, 0:T], ones[:, :], bfA[:, j, 0:T], start=(j == 0), stop=(j == 1))
        sKV = nextw()
        sAQ = nextw()
        Pq = [PS[3], PS[6], PS[7]]
        qsl = [2, 3, 0]
        for j in range(3):
            proj(Pq[j], sAQ, 0, 384, j, T, xbf, 8)
            K.act(bfA[:, qsl[j], 0:T], Pq[j][:, 0:T], AF.Square)
        Pstq = PS[5]
        for j in range(3):
            K.mm(Pstq[:, 0:T], ones[:, :], bfA[:, qsl[j], 0:T], start=(j == 0), stop=(j == 2))
        if KSUB <= 1:
            return
        rstd_from(Pst, T, 1.0 / 256, RMS_EPS)
        if KSUB <= 2:
            return
        for j in range(2):
            K.stt(ckv_f[:, j, 0:T], Pkv[j][:, 0:T], cvc(l, C_KVG + j), st2[:, 0:T], ALU.mult, ALU.mult)
            K.copy("pool", ckv_bf[:, j, 0:T], ckv_f[:, j, 0:T])
        if KSUB <= 3:
            return
        K.tt("dve", tAB[64:96, 0, 0:T], Pkr[64:96, 0:T], cosv, ALU.mult)
        K.tt("dve", tAB[64:96, 1, 0:T], Pkr[96:128, 0:T], sinv, ALU.mult)
        K.tt("pool", kr_f[64:96, 0:T], tAB[64:96, 0, 0:T], tAB[64:96, 1, 0:T], ALU.add)
        if KSUB <= 4:
            return
        for h in range(8):
            K.copy("pool" if h % 2 == 0 else "act", KT[64:96, h, pos0:pos0 + T], kr_f[64:96, 0:T])
        if KSUB <= 5:
            return
        expand_kv(T, pos0, kb0, sKV, ckv_bf)
        rstd_from(Pstq, T, 1.0 / 384, RMS_EPS)
        if KSUB <= 6:
            return
        for sb in range(nsub):
            nk = min(128, T - sb * 128)
            P = PS[5]
            xpose(4 + sb % 2, [ckv_f[:, j, sb * 128:sb * 128 + nk] for j in range(2)], 128, nk, outst[0:nk, sb, :])
            xpose(5 - sb % 2, [kr_f[64:96, sb * 128:sb * 128 + nk]], 32, nk, krst[0:nk, sb, :], kbase=64)
        if KSUB <= 7:
            return
        if kind == "p":
            K.dma("sp", "o_ckv", ockvp[l, s, pos0:pos0 + 512, :].rearrange("(a p) d -> p a d", p=128),
                  outst[:, :, :].ap, in_v=outst[:, :, :])
            K.dma("sp", "o_kr", okrp[l, s, pos0:pos0 + 512, :].rearrange("(a p) d -> p a d", p=128),
                  krst[:, :, :].ap, in_v=krst[:, :, :])
        else:
            K.dma("sp", "o_ckv", ockvs[l, s, :, :], outst[0:TS, 0, :].ap, in_v=outst[0:TS, 0, :])
            K.dma("sp", "o_kr", okrs[l, s, :, :], krst[0:TS, 0, :].ap, in_v=krst[0:TS, 0, :])

        if KSTOP <= 1:
            return
        for j in range(3):
            K.stt(qn_bf[:, j, 0:T], Pq[j][:, 0:T], cvc(l, C_QG + j), st2[:, 0:T], ALU.mult, ALU.mult)
        sUQ = nextw()
        nkeys = pos0 + T
        nkb = (nkeys + 127) // 128
        pti = [0]
        def emit_Q(h):
            hb = h % 2
            Pqh = PS[h % 2]
            for j in range(3):
                K.mm(Pqh[:, 0:T], W(sUQ, j * 1024 + h * 128, 128), qn_bf[:, j, 0:T], start=(j == 0), stop=(j == 2))
            K.copy("act", QT[0:64, hb, 0:T], Pqh[0:64, 0:T])
            K.tt("dve", tAB[64:96, 0, 0:T], Pqh[64:96, 0:T], cosv, ALU.mult)
            K.tt("dve", tAB[64:96, 1, 0:T], Pqh[96:128, 0:T], sinv, ALU.mult)
            K.tt("pool", QT[64:96, hb, 0:T], tAB[64:96, 0, 0:T], tAB[64:96, 1, 0:T], ALU.add)

        emit_Q(0)
        pend = [None]
        for h in range(8):
            hb = h % 2
            Pnum = PS[2] if hb == 0 else PS[6]
            Pden = PS[3] if hb == 0 else PS[7]
            kbl = []
            for kb in range(nkb):
                nk = min(128, nkeys - kb * 128)
                jd = kb - kb0 if kind == "p" else -1
                c0 = 128 * jd if jd > 0 else 0
                kbl.append((kb, nk, jd, c0, 4 + (pti[0] % 2), pti[0] % 3))
                pti[0] += 1

            def S_(idx):
                kb, nk, jd, c0, sbk, p3 = kbl[idx]
                Ps = PS[sbk]
                K.mm(Ps[0:nk, c0:T], KT[0:96, h, kb * 128:kb * 128 + nk], QT[0:96, hb, c0:T])
                K.act(PT[0:nk, p3, c0:T], Ps[0:nk, c0:T], AF.Exp, scale=ATTN_SCALE)
                if jd >= 0:
                    K.act(PT[64:128, p3, c0:c0 + 64], PT[64:128, p3, c0:c0 + 64], AF.Identity, scale=0.0)

            def PV_(idx):
                kb, nk, jd, c0, sbk, p3 = kbl[idx]
                K.mm(Pnum[0:65, c0:T], Vb[0:nk, kb, h, :], PT[0:nk, p3, c0:T],
                     start=(idx == 0), stop=(idx == nkb - 1))

            S_(0)
            if h + 1 < 8:
                emit_Q(h + 1)
            if pend[0] is not None:
                pend[0][0]()
            for idx in range(1, nkb):
                S_(idx)
                PV_(idx - 1)
                if idx == min(3, nkb - 1) and pend[0] is not None:
                    pend[0][1]()
                    pend[0] = None
            PV_(nkb - 1)
            if pend[0] is not None:
                pend[0][1]()
                pend[0] = None

            def normA_(h=h, hb=hb, Pnum=Pnum, Pden=Pden):
                K.copy("act", d32[64:65, 0:T], Pnum[64:65, 0:T])
                K.copy("act", dhl[64:65, 0, 0:T], Pnum[64:65, 0:T])
                K.tt("dve", dhl[64:65, 1, 0:T], d32[64:65, 0:T], dhl[64:65, 0, 0:T], ALU.subtract)

            def normB_(h=h, hb=hb, Pnum=Pnum, Pden=Pden):
                K.mm(Pden[0:64, 0:T], ones[64:65, 0:64], dhl[64:65, 0, 0:T], start=True, stop=False)
                K.mm(Pden[0:64, 0:T], ones[64:65, 0:64], dhl[64:65, 1, 0:T], start=False, stop=True)
                K.add("dve", lambda e, o=recb[0:64, 0:T].ap, i=Pden[0:64, 0:T].ap: e.reciprocal(o, i),
                      [Pden[0:64, 0:T]], [recb[0:64, 0:T]])
                K.tt("dve", oT[hb * 64:hb * 64 + 64, h // 2, 0:T], Pnum[0:64, 0:T], recb[0:64, 0:T], ALU.mult)

            pend[0] = (normA_, normB_)
        pend[0][0]()
        pend[0][1]()
        pend[0] = None

        if KSTOP <= 2:
            return
        sC = nextw()
        Pc = [pb(), pb(), pb(), pb()]
        for cc in range(4):
            proj(Pc[cc], sC, 0, 512, cc, T, xbf, 8)
        for c in range(2):
            K.act(tAB[:, c, 0:T], Pc[2 + c][:, 0:T], AF.Tanh, scale=0.5)
            K.ts("dve", tAB[:, c, 0:T], tAB[:, c, 0:T], 0.5, ALU.mult, 0.5, ALU.add)
            K.tt("dve", upf[:, c, 30:30 + T], tAB[:, c, 0:T], Pc[c][:, 0:T], ALU.mult)
            K.copy("pool", up_bf[:, c, 0:30 + T], upf[:, c, 0:30 + T])
        Pcv = [PS[6], PS[7]]
        for c in range(2):
            sCD = nextw()
            for j in range(31):
                K.mm(Pcv[c][:, 0:T], W(sCD, j * 128, 128), up_bf[:, c, j:j + T], start=(j == 0), stop=(j == 30))
        for c in range(2):
            K.act(cvs[:, c, 0:T], Pcv[c][:, 0:T], AF.Identity, bias=cvc(l, C_CDB + c))
            K.act(bfA[:, c, 0:T], Pcv[c][:, 0:T], AF.Identity, bias=cvc(l, C_CDB + c))
            K.act(bfA[:, 2 + c, 0:T], Pcv[c][:, 0:T], AF.Square, bias=cvc(l, C_CDB + c))
        if KSTOP <= 3:
            return
        sS0 = nextw()
        sS1 = nextw()
        Pg = [pb(), pb(), pb(), pb()]
        for cc in range(4):
            proj(Pg[cc], sS0, 0, 512, cc, T, xbf, 8)
        for c in range(2):
            K.copy("act", gb_sb[:, c, 0:T], Pg[c][:, 0:T])
        Ph = [PS[4], PS[5]]
        for c in range(2):
            proj(Ph[c], sS1, 0, 256, c, T, xbf, 8)
            K.copy("act", h_sb[:, c, 0:T], Ph[c][:, 0:T])
            K.tt("dve", zpf[:, c, 2:2 + T], Pg[2 + c][:, 0:T], h_sb[:, c, 0:T], ALU.mult)
            K.copy("pool", zp_bf[:, c, 0:2 + T], zpf[:, c, 0:2 + T])
        for c in range(2):
            Pz = pb()
            for j in range(3):
                K.mm(Pz[:, 0:T], W(sS1, 2048 + (c * 3 + j) * 128, 128), zp_bf[:, c, j:j + T], start=(j == 0), stop=(j == 2))
            K.tt("dve", scin[:, c, 0:T], Pz[:, 0:T], gb_sb[:, c, 0:T], ALU.mult)

        P1, P2 = PS[4], PS[5]
        for c in range(2):
            K.mm(P1[:, 0:T], ones[:, :], bfA[:, c, 0:T], start=(c == 0), stop=(c == 1))
        for c in range(2):
            K.mm(P2[:, 0:T], ones[:, :], bfA[:, 2 + c, 0:T], start=(c == 0), stop=(c == 1))
        ln_stats(P1, P2, T, 256)
        for c in range(2):
            K.tt("dve", cvs[:, c, 0:T], cvs[:, c, 0:T], st1[:, 0:T], ALU.subtract)
            K.tt("dve", cvs[:, c, 0:T], cvs[:, c, 0:T], st2[:, 0:T], ALU.mult)
            K.act(tAB[:, c, 0:T], cvs[:, c, 0:T], AF.Tanh, bias=cvc(l, C_HCLB + c), scale=cvc(l, C_HCLG + c))
            K.act(cvs[:, c, 0:T], cvs[:, c, 0:T], AF.Identity, bias=cvc(l, C_CLB + c), scale=cvc(l, C_CLG + c))
            K.ts("dve", tAB[:, c, 0:T], tAB[:, c, 0:T], 0.5, ALU.mult, 0.5, ALU.add)
            K.tt("dve", conf_act[:, c, 0:T], cvs[:, c, 0:T], tAB[:, c, 0:T], ALU.mult)

        if kind == "s" or i == 3:
            xpose(4, [upf[:, c, T:T + 30] for c in range(2)], 128, 30, sst[0:30, :])
            K.dma("sp", "o_st", (oconfp if kind == "p" else oconfs)[l, s, :, :], sst[0:30, :].ap, in_v=sst[0:30, :])
            xpose(5, [zpf[:, c, T:T + 2] for c in range(2)], 128, 2, sst[0:2, :])
            K.dma("sp", "o_st", (oscp if kind == "p" else oscs)[l, s, :, :], sst[0:2, :].ap, in_v=sst[0:2, :])
        elif kind == "p":
            for c in range(2):
                K.copy("pool", upf[:, c, 0:30], upf[:, c, T:T + 30])
                K.copy("pool", zpf[:, c, 0:2], zpf[:, c, T:T + 2])

        if KSTOP <= 4:
            return
        sMs = {}
        pcnt = [0]

        def do_pair(c, b):
            sM = sMs[c]
            mi = c % 2
            Pgt = pb()
            Pbr = pb()
            proj(Pgt, sM, 1024 + b * 1024, 128, 0, T, xbf, 8)
            if b == 0:
                proj(Pbr, sM, 0, 128, 0, T, oT, 4)
            elif b == 1:
                proj(Pbr, sM, 512, 128, 0, T, conf_act, 2)
            else:
                proj(Pbr, sM, 768, 128, 0, T, scin, 2)
            ti2 = pcnt[0] % 2
            pcnt[0] += 1
            K.act(t_sb[:, ti2, 0:T], Pgt[:, 0:T], AF.Tanh, bias=cvc(l, C_HBG + b * 8 + c), scale=0.5)
            if b == 0:
                K.stt(macc[:, mi, 0:T], t_sb[:, ti2, 0:T], 1.0, Pbr[:, 0:T], ALU.add, ALU.mult)
            else:
                K.stt(mtmp[:, ti2, 0:T], t_sb[:, ti2, 0:T], 1.0, Pbr[:, 0:T], ALU.add, ALU.mult)
                K.tt("pool", macc[:, mi, 0:T], macc[:, mi, 0:T], mtmp[:, ti2, 0:T], ALU.add)

        for c in range(8):
            sMs[c] = nextw()
            do_pair(c, 0)
            do_pair(c, 2)
            if c > 0:
                do_pair(c - 1, 1)
                K.act(merged_bf[:, c - 1, 0:T], macc[:, (c - 1) % 2, 0:T], AF.Identity, scale=0.5)
        do_pair(7, 1)
        K.act(merged_bf[:, 7, 0:T], macc[:, 1, 0:T], AF.Identity, scale=0.5)

        if KSTOP <= 5:
            return
        P1, P2 = PS[4], PS[5]

        def stat_mm(c):
            zi = c % 2
            K.mm(P1[:, 0:T], ones[:, :], bfA[:, zi, 0:T], start=(c == 0), stop=(c == 7))
            K.mm(P2[:, 0:T], ones[:, :], bfA[:, 2 + zi, 0:T], start=(c == 0), stop=(c == 7))

        for c in range(8):
            if c % 4 == 0:
                sX = nextw()
            Pm = pb()
            proj(Pm, sX, 0, 512, c % 4, T, merged_bf, 8)
            K.stt(xv(c), xv(c), DN_ALPHA, Pm[:, 0:T], ALU.mult, ALU.add)
            zi = c % 2
            K.copy("act", bfA[:, zi, 0:T], xv(c))
            K.act(bfA[:, 2 + zi, 0:T], xv(c), AF.Square)
            if c > 0:
                stat_mm(c - 1)
        stat_mm(7)
        ln_stats(P1, P2, T, 1024)
        layernorm_apply(xv, T, l, C_L1G, C_L1B, xbf)

        if KSTOP <= 6:
            return
        for a in range(2):
            for j in range(4):
                sF = nextw()
                for cc in range(4):
                    hc = j * 4 + cc
                    Phh = pb()
                    proj(Phh, sF, 0, 512, cc, T, xbf, 8)
                    ri = hc % 2
                    K.act(tAB[:, ri, 0:T], Phh[:, 0:T], AF.Relu, bias=cvc(l, C_BF1 + a * 16 + hc))
                    K.tt(alt(), hbuf[:, hc, 0:T], tAB[:, ri, 0:T], tAB[:, ri, 0:T], ALU.mult)
            if a == 1 and nxt is not None:
                for c in range(8):
                    K.copy(alt(), xbf[:, c, 0:512], xres[:, nxt, c, 0:512])
            for j in range(4):
                sF = nextw()
                for cc in range(2):
                    c = j * 2 + cc
                    Py = pb()
                    proj(Py, sF, 0, 256, cc, T, hbuf, 16)
                    if a == 0:
                        ri = c % 2
                        K.act(tAB[:, ri, 0:T], Py[:, 0:T], AF.Identity, bias=cvc(l, C_BF2 + c))
                        K.stt(xv(c), xv(c), DN_ALPHA, tAB[:, ri, 0:T], ALU.mult, ALU.add)
                    else:
                        K.tt("dve", xv(c), xv(c), Py[:, 0:T], ALU.add)
                        zi = c % 2
                        K.copy("act", bfA[:, zi, 0:T], xv(c))
                        K.act(bfA[:, 2 + zi, 0:T], xv(c), AF.Square)
                        if c > 0:
                            stat_mm(c - 1)
        stat_mm(7)
        ln_stats(P1, P2, T, 1024)
        layernorm_apply(xv, T, l, C_L2G, C_L2B, None)

        if l == nl - 1:
            for g in range(2):
                for sb in range(nsub):
                    nk = min(128, T - sb * 128)
                    xpose((g * 4 + sb) % 4, [xres[:, ti, g * 4 + cc, sb * 128:sb * 128 + nk] for cc in range(4)], 128, nk,
                          xst[0:nk, sb, g * 512:(g + 1) * 512])
            if kind == "p":
                K.dma("sp", "yout", yp[s, pos0:pos0 + 512, :].rearrange("(a p) d -> p a d", p=128),
                      xst[:, :, :].ap, in_v=xst[:, :, :])
            else:
                K.dma("sp", "yout", ys[s, :, :], xst[0:TS, 0, :].ap, in_v=xst[0:TS, 0, :])

    def ingest_cache(s, l):
        slot = wget()
        K.dma("sp", "c_ckv", cst[:, :, :].ap, cckv[l, s, :, :].rearrange("(a p) d -> p a d", p=128), out_v=cst[:, :, :])
        if os.environ.get('KV1', '1') == '1':
            K.memset("dve", kst[:, :, :], 0.0)
        K.dma("sp", "c_kr", kst[:, :, 64:96].ap, ckr[l, s, :, :].rearrange("(a p) d -> p a d", p=128), out_v=kst[:, :, 64:96])
        if KING <= 1:
            return
        cbf = sub("cbf", 10240, [128, 4, 256], BF16)
        kbf = sub("kbf", 12288, [128, 4, 96], BF16)
        for g in range(4):
            K.copy("pool", cbf[:, :, :], cst[:, g * 4:(g + 1) * 4, :])
            K.copy("pool", kbf[:, :, :], kst[:, g * 4:(g + 1) * 4, :])
            if KING <= 3:
                continue
            bj = [(2 * g) % 4, (2 * g + 1) % 4]
            for kk in range(4):
                for j in range(2):
                    K.tr(PSb[bj[j]][:, kk * 128:(kk + 1) * 128], cbf[:, kk, j * 128:(j + 1) * 128], identb[:, :])
            K.copy("act", ckv_bf2[:, 0, 0:512], PSb[bj[0]][:, 0:512])
            K.copy("dve", ckv_bf2[:, 1, 0:512], PSb[bj[1]][:, 0:512])
            if KING <= 4:
                continue
            expand_kv(512, g * 512, g * 4, slot, ckv_bf2)
            if KING <= 5:
                continue
            Pk = PSb[4 + g % 2]
            for kk in range(4):
                K.tr(Pk[0:96, kk * 128:(kk + 1) * 128], kbf[:, kk, :], identb[:, :])
            if os.environ.get('KV2', '0') == '1':
                continue
            ktmp = sub("ktmp", 9216, [128, 512], BF16)
            K.copy("act", ktmp[0:96, :], Pk[0:96, 0:512])
            for h in range(8):
                K.copy("dve" if h % 2 == 0 else "pool", KT[64:96, h, g * 512:(g + 1) * 512], ktmp[64:96, :])
        if KING <= 6:
            return
        K.dma("sp", "st_in", sst[0:30, :].ap, sconf[l, s, :, :], out_v=sst[0:30, :])
        for c in range(2):
            xpose(c, [sst[0:30, c * 128:(c + 1) * 128]], 30, 128, upf[:, c, 0:30])
        K.dma("sp", "st_in", sst[0:2, :].ap, ssc[l, s, :, :], out_v=sst[0:2, :])
        for c in range(2):
            xpose(2 + c, [sst[0:2, c * 128:(c + 1) * 128]], 2, 128, zpf[:, c, 0:2])

    descs = []
    for (kind, s_) in units:
        for l in range(nl):
            if kind == "p":
                for i in range(min(4, KTILES)):
                    descs.append((kind, s_, l, i))
            else:
                descs.append((kind, s_, l, 0))
    pre = False
    for di, (kind, s_, l, i) in enumerate(descs):
        if kind == "p" and i == 0:
            for c in range(2):
                K.memset("pool", upf[:, c, 0:30], 0.0)
                K.memset("pool", zpf[:, c, 0:2], 0.0)
        fbc = None
        if kind == "p" and s_ == 0 and i >= 1 and l + 1 < nl:
            fbc = (l + 2) * NBLK
        nxt = None
        if di + 1 < len(descs):
            nk_, ns_, nl_, ni_ = descs[di + 1]
            if kind == "p" and nk_ == "p" and nl_ > 0 and KSTOP > 7:
                nxt = ni_
        tile(kind, s_, l, i, fbc, nl, pre_xbf=pre, nxt=nxt)
        pre = nxt is not None
    K.emit()
    K.close()
    return nc, K

def _chunkK(M):
    Kd, N = M.shape
    nk = Kd // 128
    return np.ascontiguousarray(M.reshape(nk, 128, N).transpose(1, 0, 2)).reshape(128, nk * N)


def _pack_weights(w_in, w_uq, w_uk, w_uv, w_mla_out, conf_dw_w, w_conf_out, sc_dw_w, w_sc_out, w_mix_out,
                  w_ff1, w_ff2):
    out = np.zeros((NL * NBLK * 128, BW), np.float32)
    swap = np.concatenate([np.arange(16, 32), np.arange(0, 16)])
    eye = np.eye(128, dtype=np.float32)
    for l in range(NL):
        Win = w_in[l]

        def put(b, arr):
            r0 = (l * NBLK + b) * 128
            out[r0:r0 + 128, 0:arr.shape[1]] = arr

        kr = np.zeros((DM, 128), np.float32)
        kr[:, 64:96] = Win[:, 640:672]
        kr[:, 96:128] = Win[:, 640:672][:, swap]
        put(B_AKV, _chunkK(np.concatenate([Win[:, 384:640], kr], axis=1)))
        kv = np.concatenate([w_uk[l].reshape(256, 512), w_uv[l].reshape(256, 512)], axis=1)
        put(B_KV, _chunkK(kv))
        put(B_AQ, _chunkK(Win[:, 0:384]))
        uq = np.zeros((384, 8, 128), np.float32)
        uq[:, :, 0:96] = w_uq[l]
        uq[:, :, 96:128] = w_uq[l][:, :, 64:96][:, :, swap]
        put(B_UQ, _chunkK(uq.reshape(384, 1024)))
        put(B_C, _chunkK(Win[:, 672:1184]))
        for c in range(2):
            d = np.zeros((128, 31, 128), np.float32)
            for j in range(31):
                d[:, j, :] = eye * conf_dw_w[l][j, c * 128:(c + 1) * 128][:, None]
            put(B_CD0 + c, d.reshape(128, 31 * 128))
        put(B_S0, _chunkK(Win[:, 1184:1696]))
        d = np.zeros((128, 6, 128), np.float32)
        for c in range(2):
            for j in range(3):
                d[:, c * 3 + j, :] = eye * sc_dw_w[l][j, c * 128:(c + 1) * 128][:, None]
        put(B_S1, np.concatenate([_chunkK(Win[:, 1696:1952]), d.reshape(128, 768)], axis=1))
        for c in range(8):
            cs = slice(c * 128, (c + 1) * 128)
            parts = [_chunkK(w_mla_out[l][:, cs]), _chunkK(w_conf_out[l][:, cs]), _chunkK(w_sc_out[l][:, cs])]
            for b in range(3):
                parts.append(_chunkK(Win[:, 1952 + b * 1024 + c * 128:1952 + b * 1024 + (c + 1) * 128]))
            put(B_M0 + c, np.concatenate(parts, axis=1))
        for g in range(2):
            put(B_X0 + g, _chunkK(w_mix_out[l][:, g * 512:(g + 1) * 512]))
        for a in range(2):
            for j in range(4):
                put((B_F1A if a == 0 else B_F1B) + j, _chunkK(w_ff1[l][:, a * 2048 + j * 512:a * 2048 + (j + 1) * 512]))
                put((B_F2A if a == 0 else B_F2B) + j, _chunkK(w_ff2[l][a * 2048:(a + 1) * 2048, j * 256:(j + 1) * 256]))
    return out


def _pack_vecs(b_gate, q_norm_g, kv_norm_g, conf_dw_b, conf_ln_g, conf_ln_b, ln1_g, ln1_b, b_ff1, b_ff2, ln2_g, ln2_b):
    cv = np.zeros((128, NL * NCOL), np.float32)

    def cols(v):
        return np.ascontiguousarray(v.reshape(-1, 128).T)

    for l in range(NL):
        o = l * NCOL
        cv[:, o + C_BG:o + C_BG + 24] = cols(b_gate[l].reshape(-1))
        cv[:, o + C_QG:o + C_QG + 3] = cols(q_norm_g[l])
        cv[:, o + C_KVG:o + C_KVG + 2] = cols(kv_norm_g[l])
        cv[:, o + C_CDB:o + C_CDB + 2] = cols(conf_dw_b[l])
        cv[:, o + C_CLG:o + C_CLG + 2] = cols(conf_ln_g[l])
        cv[:, o + C_CLB:o + C_CLB + 2] = cols(conf_ln_b[l])
        cv[:, o + C_L1G:o + C_L1G + 8] = cols(ln1_g[l])
        cv[:, o + C_L1B:o + C_L1B + 8] = cols(ln1_b[l])
        cv[:, o + C_BF1:o + C_BF1 + 32] = cols(b_ff1[l])
        cv[:, o + C_BF2:o + C_BF2 + 8] = cols(b_ff2[l])
        cv[:, o + C_L2G:o + C_L2G + 8] = cols(ln2_g[l])
        cv[:, o + C_L2B:o + C_L2B + 8] = cols(ln2_b[l])
    return cv


def _rope_table():
    half = 16
    inv = (np.float32(10000.0) ** (-np.arange(half, dtype=np.float32) / np.float32(half))).astype(np.float32)
    pos = np.arange(NPOS, dtype=np.float32)
    ang = (pos[None, :] * inv[:, None]).astype(np.float32)
    cos = np.cos(ang).astype(np.float32)
    sin = np.sin(ang).astype(np.float32)
    t = np.zeros((128, NPOS), np.float32)
    t[64:80] = cos
    t[80:96] = cos
    t[96:112] = -sin
    t[112:128] = sin
    return t


_PROG = {}


def _get_prog(nsp=NSP, nss=NSS, nl=NL):
    key = (nsp, nss, nl)
    if key not in _PROG:
        _PROG[key] = build_program(nsp, nss, nl)[0]
    return _PROG[key]


def _make_in_maps(x_prompt, x_sample, cache_ckv, cache_krope, state_conf, state_sc, wf, cvv, ropet):
    f = lambda a: np.ascontiguousarray(np.asarray(a, dtype=np.float32))
    ident = np.eye(128, dtype=np.float32)
    maps = []
    for k in range(8):
        maps.append({
            "xp": f(x_prompt[k * NSP:(k + 1) * NSP]),
            "xs": f(x_sample[k * NSS:(k + 1) * NSS]),
            "cckv": f(cache_ckv[:, k * NSS:(k + 1) * NSS]),
            "ckr": f(cache_krope[:, k * NSS:(k + 1) * NSS]),
            "sconf": f(state_conf[:, k * NSS:(k + 1) * NSS]),
            "ssc": f(state_sc[:, k * NSS:(k + 1) * NSS]),
            "wf32": wf, "cvec": cvv, "rope": ropet, "identd": ident,
        })
    return maps


def kernel(x_prompt, x_sample, cache_ckv, cache_krope, state_conf, state_sc, w_in, b_gate, q_norm_g, w_uq,
           kv_norm_g, w_uk, w_uv, w_mla_out, conf_dw_w, conf_dw_b, conf_ln_g, conf_ln_b, w_conf_out, sc_dw_w,
           w_sc_out, w_mix_out, ln1_g, ln1_b, w_ff1, b_ff1, w_ff2, b_ff2, ln2_g, ln2_b, _cfg=None, _ncores=8):
    A = lambda a: np.asarray(a, dtype=np.float32)
    wf = _pack_weights(A(w_in), A(w_uq), A(w_uk), A(w_uv), A(w_mla_out), A(conf_dw_w), A(w_conf_out), A(sc_dw_w),
                       A(w_sc_out), A(w_mix_out), A(w_ff1), A(w_ff2))
    cvv = _pack_vecs(A(b_gate), A(q_norm_g), A(kv_norm_g), A(conf_dw_b), A(conf_ln_g), A(conf_ln_b), A(ln1_g),
                     A(ln1_b), A(b_ff1), A(b_ff2), A(ln2_g), A(ln2_b))
    maps = _make_in_maps(A(x_prompt), A(x_sample), A(cache_ckv), A(cache_krope), A(state_conf), A(state_sc),
                         wf, cvv, _rope_table())
    nc = _get_prog(*(_cfg or (NSP, NSS, NL)))
    res = run_bass_kernel_spmd(nc, maps[:_ncores], core_ids=list(range(_ncores)))
    R = res.results
    cat0 = lambda n: np.concatenate([np.asarray(r[n]) for r in R], axis=0)
    cat1 = lambda n: np.concatenate([np.asarray(r[n]) for r in R], axis=1)
    return (cat0("yp"), cat0("ys"), cat1("ockvp"), cat1("okrp"), cat1("oconfp"), cat1("oscp"),
            cat1("ockvs"), cat1("okrs"), cat1("oconfs"), cat1("oscs"))
```

```python
import numpy as np
import concourse.bass as bass
import concourse.mybir as mybir
from concourse.bass_utils import run_bass_kernel_spmd

F32 = mybir.dt.float32
BF16 = mybir.dt.bfloat16
AF = mybir.ActivationFunctionType
ALU = mybir.AluOpType
ESZ = {F32: 4, BF16: 2}


class V:
    __slots__ = ("ap", "key", "lo", "hi")

    def __init__(self, ap, key, lo, hi):
        self.ap, self.key, self.lo, self.hi = ap, key, lo, hi


class Buf:
    def __init__(self, ap, key, shape, dtype, off=0):
        self.ap, self.key, self.shape, self.dtype, self.off = ap, key, list(shape), dtype, off
        self.es = ESZ[dtype]
        st = [1] * len(shape)
        for i in range(len(shape) - 2, 0, -1):
            st[i] = st[i + 1] * shape[i + 1]
        self.st = st

    def __getitem__(self, idx):
        if not isinstance(idx, tuple):
            idx = (idx,)
        idx = list(idx) + [slice(None)] * (len(self.shape) - len(idx))
        lo = 0
        hi = 0
        for d in range(1, len(self.shape)):
            i = idx[d]
            if isinstance(i, slice):
                a = 0 if i.start is None else i.start
                b = self.shape[d] if i.stop is None else i.stop
            else:
                a, b = i, i + 1
            assert 0 <= a < b <= self.shape[d], (self.key, idx, self.shape)
            lo += a * self.st[d]
            hi += (b - 1) * self.st[d]
        if getattr(self, "whole", False):
            return V(self.ap[tuple(idx)], self.key, 0, 1 << 30)
        return V(self.ap[tuple(idx)], self.key, self.off + lo * self.es, self.off + (hi + 1) * self.es)


class Op:
    __slots__ = ("eng", "fn", "deps", "signal", "stream", "cnt", "idx")


class Kern:
    ENGS = ("pe", "act", "dve", "pool", "sp")

    def __init__(self, nc):
        self.nc = nc
        self.ops = {e: [] for e in self.ENGS}
        self.rec = {}
        self.streams = {}
        self.ctx = []

    def sbuf(self, name, shape, dtype):
        cm = self.nc.sbuf_tensor(name, list(shape), dtype)
        t = cm.__enter__()
        self.ctx.append(cm)
        b = Buf(t[tuple(slice(None) for _ in shape)], name, shape, dtype)
        b.t = t
        return b

    def psum(self, name, shape, dtype=F32):
        cm = self.nc.psum_tensor(name, list(shape), dtype)
        t = cm.__enter__()
        self.ctx.append(cm)
        b = Buf(t[tuple(slice(None) for _ in shape)], name, shape, dtype)
        b.t = t
        b.whole = True
        return b

    def sub(self, parent, name, off_bytes, shape, dtype):
        n = int(np.prod(shape[1:]))
        es_p = parent.es
        nb = n * ESZ[dtype]
        assert off_bytes % 4 == 0 and off_bytes + nb <= parent.shape[1] * es_p, (name, off_bytes, nb)
        ap = parent.ap[0:shape[0], off_bytes // es_p:(off_bytes + nb) // es_p]
        if dtype != parent.dtype:
            ap = ap.bitcast(dtype)
        if len(shape) == 3:
            ap = ap.rearrange("p (a b) -> p a b", b=shape[2])
        return Buf(ap, parent.key, shape, dtype, off=parent.off + off_bytes)

    def add(self, eng, fn, reads=(), writes=(), stream=None):
        op = Op()
        op.eng, op.fn, op.signal, op.stream = eng, fn, False, stream
        lst = self.ops[eng]
        op.idx = len(lst)
        deps = set()
        for v in reads:
            for r in self.rec.get(v.key, ()):
                if r[2] == "W" and r[0] < v.hi and v.lo < r[1]:
                    deps.add(r[3])
        for v in writes:
            for r in self.rec.get(v.key, ()):
                if r[0] < v.hi and v.lo < r[1]:
                    deps.add(r[3] if r[2] == "W" else ("war",) + r[3])
        if stream is not None:
            self.streams[stream] = self.streams.get(stream, 0) + 1
            op.cnt = self.streams[stream]
            who = ("dma", stream, op.cnt)
        else:
            who = ("eng", eng, op.idx)
        fdeps = set()
        for d in deps:
            war = d[0] == "war"
            if war:
                d = d[1:]
            if d[0] == "dma" and d[1].startswith("cast"):
                d = ("dma", d[1], self.streams[d[1]])
            if d[0] == "eng" and d[1] == eng and stream is None:
                if eng == "pe":
                    continue
            fdeps.add(d)
        op.deps = fdeps
        for v in reads:
            L = self.rec.setdefault(v.key, [])
            for r in L:
                if r[2] == "R" and r[0] == v.lo and r[1] == v.hi and r[3][0] == who[0] and r[3][1] == who[1]:
                    r[3] = who
                    break
            else:
                L.append([v.lo, v.hi, "R", who])
        for v in writes:
            L = self.rec.setdefault(v.key, [])
            L[:] = [r for r in L if not (v.lo <= r[0] and r[1] <= v.hi)]
            L.append([v.lo, v.hi, "W", who])
        lst.append(op)
        return op

    def dma(self, q, stream, out, in_, out_v=None, in_v=None):
        reads = [in_v] if in_v is not None else []
        writes = [out_v] if out_v is not None else []
        return self.add(q, lambda e: e.dma_start(out=out, in_=in_), reads, writes, stream=stream)

    def emit(self):
        nc = self.nc
        for e in self.ENGS:
            for op in self.ops[e]:
                for d in op.deps:
                    if d[0] == "eng":
                        self.ops[d[1]][d[2]].signal = True
        sigcnt = {}
        for e in self.ENGS:
            c = 0
            arr = []
            for op in self.ops[e]:
                if op.signal and op.stream is None:
                    c += 1
                arr.append(c)
            sigcnt[e] = arr
        sems = {}
        for e in self.ENGS:
            cm = nc.semaphore("s_" + e)
            sems[("eng", e)] = cm.__enter__()
            self.ctx.append(cm)
        for s in self.streams:
            cm = nc.semaphore("d_" + s)
            sems[("dma", s)] = cm.__enter__()
            self.ctx.append(cm)
        total_waits = [0]

        def emit_eng(e, eng):
            waited = {}
            for op in self.ops[e]:
                need = {}
                for d in op.deps:
                    if d[0] == "eng":
                        k = ("eng", d[1])
                        val = sigcnt[d[1]][d[2]]
                    else:
                        k = ("dma", d[1])
                        val = 16 * d[2]
                    if val > need.get(k, 0):
                        need[k] = val
                for k, val in need.items():
                    if val > waited.get(k, 0):
                        eng.wait_ge(sems[k], val)
                        waited[k] = val
                        total_waits[0] += 1
                ins = op.fn(eng)
                if op.stream is not None:
                    ins.then_inc(sems[("dma", op.stream)], 16)
                elif op.signal:
                    ins.then_inc(sems[("eng", e)], 1)
            if e == "sp":
                for s, c in self.streams.items():
                    eng.wait_ge(sems[("dma", s)], 16 * c)

        with nc.Block() as block:
            @block.tensor
            def _(eng):
                emit_eng("pe", eng)

            @block.scalar
            def _(eng):
                emit_eng("act", eng)

            @block.vector
            def _(eng):
                emit_eng("dve", eng)

            @block.gpsimd
            def _(eng):
                emit_eng("pool", eng)

            @block.sync
            def _(eng):
                emit_eng("sp", eng)
        self.total_waits = total_waits[0]

    def close(self):
        for cm in reversed(self.ctx):
            cm.__exit__(None, None, None)
        self.ctx = []

    def mm(self, out, lhsT, rhs, start=True, stop=True):
        return self.add("pe", lambda e: e.matmul(out.ap, lhsT.ap, rhs.ap, start=start, stop=stop),
                        [lhsT, rhs], [out])

    def tr(self, out, in_, ident):
        return self.add("pe", lambda e: e.transpose(out.ap, in_.ap, ident.ap), [in_, ident], [out])

    def act(self, out, in_, func, bias=None, scale=None, eng="act"):
        kw = {}
        rd = [in_]
        if bias is not None:
            if isinstance(bias, V):
                kw["bias"] = bias.ap
                rd.append(bias)
            else:
                kw["bias"] = bias
        if scale is not None:
            if isinstance(scale, V):
                kw["scale"] = scale.ap
                rd.append(scale)
            else:
                kw["scale"] = scale
        return self.add("act", lambda e: e.activation(out.ap, in_.ap, func, **kw), rd, [out])

    def tt(self, eng, out, in0, in1, op):
        return self.add(eng, lambda e: e.tensor_tensor(out.ap, in0.ap, in1.ap, op), [in0, in1], [out])

    def ts(self, eng, out, in0, s1, op0, s2=None, op1=None):
        rd = [in0]
        a1 = s1
        a2 = s2
        if isinstance(s1, V):
            rd.append(s1)
            a1 = s1.ap
        if isinstance(s2, V):
            rd.append(s2)
            a2 = s2.ap
        if op1 is None:
            return self.add(eng, lambda e: e.tensor_scalar(out.ap, in0.ap, a1, None, op0), rd, [out])
        return self.add(eng, lambda e: e.tensor_scalar(out.ap, in0.ap, a1, a2, op0, op1), rd, [out])

    def stt(self, out, in0, s, in1, op0, op1):
        rd = [in0, in1]
        a = s
        if isinstance(s, V):
            rd.append(s)
            a = s.ap
        return self.add("dve", lambda e: e.scalar_tensor_tensor(out.ap, in0.ap, a, in1.ap, op0, op1), rd, [out])

    def copy(self, eng, out, in_):
        if eng == "act":
            return self.act(out, in_, AF.Copy)
        return self.add(eng, lambda e: e.tensor_copy(out.ap, in_.ap), [in_], [out])

    def memset(self, eng, out, val):
        return self.add(eng, lambda e: e.memset(out.ap, val), [], [out])

NL = 4
NSP = 4
NSS = 2
SEQ = 2048
TS = 16
PAST = 2048
DM = 1024
NPOS = PAST + TS
NBLK = 35
BW = 4096
NCOL = 135
C_BG, C_QG, C_KVG, C_CDB, C_CLG, C_CLB, C_L1G, C_L1B, C_BF1, C_BF2, C_L2G, C_L2B = (
    0, 24, 27, 29, 31, 33, 35, 43, 51, 83, 91, 99)
C_HBG, C_HCLG, C_HCLB = 107, 131, 133
B_AKV, B_KV, B_AQ, B_UQ, B_C, B_CD0, B_CD1, B_S0, B_S1, B_M0, B_X0, B_X1, B_F1A, B_F2A, B_F1B, B_F2B = (
    0, 1, 2, 3, 4, 5, 6, 7, 8, 9, 17, 18, 19, 23, 27, 31)
BUSED = [3072, 2048, 3072, 3072, 4096, 3968, 3968, 4096, 2816] + [4096] * 26
ATTN_SCALE = 96.0 ** -0.5
DN_ALPHA = 8.0 ** 0.25
LN_EPS = 1e-5
RMS_EPS = 1e-6
NWS = 4


def build_program(nsp=NSP, nss=NSS, nl=NL):
    import os
    KSTOP = int(os.environ.get('KSTOP', '99'))
    KTILES = int(os.environ.get('KTILES', '4'))
    KSUB = int(os.environ.get('KSUB', '99'))
    KING = int(os.environ.get('KING', '99'))
    nc = bass.Bass("TRN2", target_bir_lowering=False)
    K = Kern(nc)

    def din(name, shape, dt=F32):
        return nc.dram_tensor(name, list(shape), dt, kind="ExternalInput").ap()

    def dout(name, shape):
        return nc.dram_tensor(name, list(shape), F32, kind="ExternalOutput").ap()

    xp = din("xp", [NSP, SEQ, DM])
    xs = din("xs", [NSS, TS, DM])
    cckv = din("cckv", [NL, NSS, PAST, 256])
    ckr = din("ckr", [NL, NSS, PAST, 32])
    sconf = din("sconf", [NL, NSS, 30, 256])
    ssc = din("ssc", [NL, NSS, 2, 256])
    wf32 = din("wf32", [NL * NBLK * 128, BW])
    cvec = din("cvec", [128, NL * NCOL])
    rope = din("rope", [128, NPOS])
    identd = din("identd", [128, 128])
    wbf = nc.dram_tensor("wbf", [NL * NBLK * 128, BW], BF16, kind="Internal").ap()
    yp = dout("yp", [NSP, SEQ, DM])
    ys = dout("ys", [NSS, TS, DM])
    ockvp = dout("ockvp", [NL, NSP, SEQ, 256])
    okrp = dout("okrp", [NL, NSP, SEQ, 32])
    oconfp = dout("oconfp", [NL, NSP, 30, 256])
    oscp = dout("oscp", [NL, NSP, 2, 256])
    ockvs = dout("ockvs", [NL, NSS, TS, 256])
    okrs = dout("okrs", [NL, NSS, TS, 32])
    oconfs = dout("oconfs", [NL, NSS, 30, 256])
    oscs = dout("oscs", [NL, NSS, 2, 256])

    xres = K.sbuf("xres", [128, 4, 8, 512], F32)
    KT = K.sbuf("KT", [128, 8, NPOS], BF16)
    Vb = K.sbuf("Vb", [128, 17, 8, 65], BF16)
    wbuf = K.sbuf("wbuf", [128, NWS, BW], BF16)
    xbf = K.sbuf("xbf", [128, 8, 512], BF16)
    upf = K.sbuf("upf", [128, 2, 544], F32)
    zpf = K.sbuf("zpf", [128, 2, 516], F32)
    cv = K.sbuf("cv", [128, NL * NCOL], F32)
    ropeT = K.sbuf("ropeT", [128, 2, 512], F32)
    ident = K.sbuf("ident", [128, 128], F32)
    ones = K.sbuf("ones", [128, 128], BF16)
    mhalf = K.sbuf("mhalf", [128, 512], F32)
    mone = K.sbuf("mone", [128, 512], F32)
    epsc = K.sbuf("epsc", [128, 2], F32)
    arena = K.sbuf("arena", [128, 16384], BF16)

    def sub(name, off, shape, dt):
        return K.sub(arena, name, off, shape, dt)

    st1 = sub("st1", 0, [128, 512], F32)
    st2 = sub("st2", 2048, [128, 512], F32)
    st3 = sub("st3", 4096, [128, 512], F32)
    tAB = sub("tAB", 6144, [128, 2, 512], F32)
    bfA = sub("bfA", 10240, [128, 4, 512], BF16)
    ckv_f = sub("ckv_f", 14336, [128, 2, 512], F32)
    ckv_bf = sub("ckv_bf", 18432, [128, 2, 512], BF16)
    kr_f = sub("kr_f", 20480, [128, 512], F32)
    outst = sub("outst", 22528, [128, 4, 256], F32)
    krst = sub("krst", 26624, [128, 4, 32], F32)
    qn_bf = sub("qn_bf", 14336, [128, 3, 512], BF16)
    QT = sub("QT", 17408, [128, 2, 512], BF16)
    PT = sub("PT", 19456, [128, 3, 512], BF16)
    recb = sub("recb", 22528, [128, 512], F32)
    d32 = sub("d32", 24576, [128, 512], F32)
    dhl = sub("dhl", 26624, [128, 2, 512], BF16)
    oT = sub("oT", 28672, [128, 4, 512], BF16)
    conf_act = sub("conf_act", 26624, [128, 2, 512], BF16)
    scin = sub("scin", 24576, [128, 2, 512], BF16)
    cvs = sub("cvs", 14336, [128, 2, 512], F32)
    up_bf = sub("up_bf", 18432, [128, 2, 544], BF16)
    h_sb = sub("h_sb", 20608, [128, 2, 512], F32)
    zp_bf = sub("zp_bf", 18432, [128, 2, 516], BF16)
    gb_sb = sub("gb_sb", 6144, [128, 2, 512], F32)
    mtmp = sub("mtmp", 0, [128, 2, 512], F32)
    macc = sub("macc", 4096, [128, 2, 512], F32)
    t_sb = sub("t_sb", 8192, [128, 2, 512], F32)
    merged_bf = sub("merged_bf", 16384, [128, 8, 512], BF16)
    hbuf = sub("hbuf", 14336, [128, 16, 512], BF16)
    xst = sub("xst", 14336, [128, 4, 1024], F32)
    cst = sub("cst", 14336, [128, 16, 256], F32)
    kst = sub("kst", 0, [128, 16, 96], F32)
    sst = sub("sst", 8192, [128, 256], F32)

    PS = [K.psum("ps%d" % i, [128, 512]) for i in range(8)]
    psrot = [0]

    def pb(pool=(0, 1, 2, 3)):
        psrot[0] += 1
        return PS[pool[psrot[0] % len(pool)]]

    identb = K.sbuf("identb", [128, 128], BF16)
    PSb = []
    for i in range(8):
        bb = Buf(PS[i].t[:, :].bitcast(BF16), PS[i].key, [128, 1024], BF16)
        bb.whole = True
        PSb.append(bb)

    def planes(v):
        a = v.ap.bitcast(BF16).rearrange("p (n t) -> p n t", t=2)
        return (V(a[:, :, 0], v.key, v.lo, v.hi), V(a[:, :, 1], v.key, v.lo, v.hi))

    def xpose(bank, srcs, Kp, M, dst, kbase=0):
        Pb = PSb[bank]
        off = 0
        idv = identb[kbase:kbase + Kp, kbase:kbase + Kp]
        for src in srcs:
            lo, hi = planes(src)
            K.tr(Pb[0:M, off:off + Kp], lo, idv)
            K.tr(Pb[0:M, 512 + off:512 + off + Kp], hi, idv)
            off += Kp
        dlo, dhi = planes(dst)
        K.copy("dve", dlo, Pb[0:M, 0:off])
        K.copy("dve", dhi, Pb[0:M, 512:512 + off])

    units = []
    for s in range(nsp):
        units.append(("p", s))
    for s in range(nss):
        units.append(("s", s))
    worder = []
    for (kind, s) in units:
        for l in range(nl):
            for i in range(4 if kind == "p" else 1):
                if kind == "s":
                    worder.append((l, B_KV))
                for b in range(NBLK):
                    worder.append((l, b))
    wstate = {"loaded": 0, "n": 0, "cast_next": 0, "owner": [-1] * NWS}
    cast_list = [(l, b) for l in range(nl) for b in range(NBLK)]

    def wreg(l, b):
        r = (l * NBLK + b)
        return V(None, "wbf", r * 16, r * 16 + 16)

    def do_cast(upto):
        while wstate["cast_next"] < min(upto, len(cast_list)):
            l, b = cast_list[wstate["cast_next"]]
            r0 = (l * NBLK + b) * 128
            src = wf32[r0:r0 + 128, :]
            dst = wbf[r0:r0 + 128, :]
            K.add("pool", lambda e, dst=dst, src=src: e.dma_start(out=dst, in_=src), [], [wreg(l, b)],
                  stream="cast%d" % (wstate["cast_next"] % 16))
            wstate["cast_next"] += 1

    def wload(n):
        l, b = worder[n]
        assert l * NBLK + b < wstate["cast_next"], ("cast not recorded", l, b)
        slot = n % NWS
        used = BUSED[b]
        r0 = (l * NBLK + b) * 128
        src = wbf[r0:r0 + 128, 0:used]
        dv = wbuf[:, slot, 0:used]
        wstate["owner"][slot] = n
        K.add("sp", lambda e, o=dv.ap, i=src: e.dma_start(out=o, in_=i), [wreg(l, b)], [dv], stream="w%d" % slot)

    def wget():
        n = wstate["n"]
        while wstate["loaded"] < min(n + NWS - 1, len(worder)):
            wload(wstate["loaded"])
            wstate["loaded"] += 1
        wstate["n"] = n + 1
        return n

    def W(nblk, off, n, p0=0, p1=128):
        slot = nblk % NWS
        assert wstate["owner"][slot] == nblk, ("weight slot recycled", nblk, wstate["owner"])
        return wbuf[p0:p1, slot, off:off + n]

    K.dma("sp", "cvl", cv[:, :].ap, cvec, out_v=cv[:, :])
    K.dma("sp", "idl", ident[:, :].ap, identd, out_v=ident[:, :])
    K.copy("dve", identb[:, :], ident[:, :])
    K.memset("dve", ones[:, :], 1.0)
    K.memset("dve", epsc[:, 0:1], LN_EPS)
    K.memset("dve", epsc[:, 1:2], RMS_EPS)
    K.memset("dve", Vb[:, :, :, 64:65], 1.0)
    K.memset("dve", mhalf[:, :], -0.5)
    K.memset("dve", mone[:, :], -1.0)
    for l in range(NL):
        o = l * NCOL
        K.ts("dve", cv[:, o + C_HBG:o + C_HBG + 24], cv[:, o + C_BG:o + C_BG + 24], 0.5, ALU.mult)
        K.ts("dve", cv[:, o + C_HCLG:o + C_HCLG + 4], cv[:, o + C_CLG:o + C_CLG + 4], 0.5, ALU.mult)

    def cvc(l, off):
        return cv[:, l * NCOL + off:l * NCOL + off + 1]

    do_cast(NBLK)

    rope_n = [0]
    out_eng = [0]

    def alt(engs=("dve", "pool")):
        out_eng[0] += 1
        return engs[out_eng[0] % len(engs)]

    def proj(P, slot, woff, ncols, cc, T, rhsbuf, nk, start=True, stop=True):
        for k in range(nk):
            K.mm(P[:, 0:T], W(slot, woff + k * ncols + cc * 128, 128), rhsbuf[:, k, 0:T],
                 start=(start and k == 0), stop=(stop and k == nk - 1))

    def rstd_from(Pst, T, inv_n, eps):
        K.act(st1[:, 0:T], Pst[:, 0:T], AF.Ln, bias=epsc[:, 1:2], scale=inv_n)
        K.act(st2[:, 0:T], st1[:, 0:T], AF.Exp, scale=-0.5)

    def ln_stats(P1, P2, T, n):
        K.act(st1[:, 0:T], P1[:, 0:T], AF.Identity, scale=1.0 / n)
        K.act(st3[:, 0:T], P1[:, 0:T], AF.Square, scale=1.0 / n)
        K.stt(st3[:, 0:T], P2[:, 0:T], 1.0 / n, st3[:, 0:T], ALU.mult, ALU.subtract)
        K.act(st3[:, 0:T], st3[:, 0:T], AF.Ln, bias=epsc[:, 0:1])
        K.act(st2[:, 0:T], st3[:, 0:T], AF.Exp, scale=-0.5)

    ckv_bf2 = sub("ckv_bf2", 6144, [128, 2, 512], BF16)

    def expand_kv(T, kcol0, kb0, slot, ckv_bf):
        for h in range(8):
            P = pb((0, 1, 2))
            for j in range(2):
                K.mm(P[0:64, 0:T], W(slot, j * 1024 + h * 64, 64), ckv_bf[:, j, 0:T], start=(j == 0), stop=(j == 1))
            if h % 2 == 0:
                K.copy("act", KT[0:64, h, kcol0:kcol0 + T], P[0:64, 0:T])
            else:
                K.copy("dve", KT[0:64, h, kcol0:kcol0 + T], P[0:64, 0:T])
        nsub = (T + 127) // 128
        for sb in range(nsub):
            nk = min(128, T - sb * 128)
            P = pb((0, 1, 2))
            for j in range(2):
                K.mm(P[0:nk, 0:512], ckv_bf[:, j, sb * 128:sb * 128 + nk], W(slot, j * 1024 + 512, 512),
                     start=(j == 0), stop=(j == 1))
            pv = P[0:nk, 0:512]
            pv3 = V(pv.ap.rearrange("p (h d) -> p h d", d=64), pv.key, pv.lo, pv.hi)
            K.copy("act" if sb % 2 == 0 else "dve", Vb[0:nk, kb0 + sb, :, 0:64], pv3)

    def layernorm_apply(xv, T, l, cg, cb, xbf_out):
        for c in range(8):
            K.tt("dve", xv(c), xv(c), st1[:, 0:T], ALU.subtract)
            K.tt("pool", xv(c), xv(c), st2[:, 0:T], ALU.mult)
            if xbf_out is not None:
                K.act(xbf_out[:, c, 0:T], xv(c), AF.Identity, bias=cvc(l, cb + c), scale=cvc(l, cg + c))
            K.act(xv(c), xv(c), AF.Identity, bias=cvc(l, cb + c), scale=cvc(l, cg + c))

    def tile(kind, s, l, i, first_block_cast, nl, pre_xbf=False, nxt=None):
        T = 512 if kind == "p" else TS
        pos0 = i * 512 if kind == "p" else PAST
        kb0 = pos0 // 128
        nsub = (T + 127) // 128
        ti = i if kind == "p" else 0

        def xv(c):
            return xres[:, ti, c, 0:T]

        bcount = [0]
        if kind == "s":
            ingest_cache(s, l)

        def nextw():
            if first_block_cast is not None and bcount[0] % 2 == 0:
                do_cast(wstate["cast_next"] + 1 if wstate["cast_next"] < first_block_cast else 0)
            bcount[0] += 1
            return wget()

        if l == 0:
            if kind == "p":
                src = xp[s, pos0:pos0 + 512, :].rearrange("(a p) d -> p a d", p=128)
                K.dma("sp", "xin", xst[:, :, :].ap, src, out_v=xst[:, :, :])
            else:
                K.dma("sp", "xin", xst[0:TS, 0, :].ap, xs[s, :, :], out_v=xst[0:TS, 0, :])
            nk0 = min(128, T)
            for c in range(8):
                xpose(c % 4, [xst[0:nk0, sb, c * 128:(c + 1) * 128] for sb in range(nsub)], nk0, 128, xv(c))
        if not pre_xbf:
            for c in range(8):
                K.copy(alt(), xbf[:, c, 0:T], xv(c))
        rs = rope_n[0] % 2
        rope_n[0] += 1
        K.dma("sp", "rope%d" % rs, ropeT[64:128, rs, 0:T].ap, rope[64:128, pos0:pos0 + T], out_v=ropeT[64:128, rs, 0:T])
        cosv = ropeT[64:96, rs, 0:T]
        sinv = ropeT[96:128, rs, 0:T]

        if KSTOP <= 0:
            return
        sAKV = nextw()
        Pkv = [PS[0], PS[1]]
        proj(Pkv[0], sAKV, 0, 384, 0, T, xbf, 8)
        proj(Pkv[1], sAKV, 0, 384, 1, T, xbf, 8)
        Pkr = PS[2]
        proj(Pkr, sAKV, 0, 384, 2, T, xbf, 8)
        for j in range(2):
            K.act(bfA[:, j, 0:T], Pkv[j][:, 0:T], AF.Square)
        Pst = PS[4]
        for j in range(2):
            K.mm(Pst[:, 0:T], ones[:, :], bfA[:, j, 0:T], start=(j == 0), stop=(j == 1))
        sKV = nextw()
        sAQ = nextw()
        Pq = [PS[3], PS[6], PS[7]]
        qsl = [2, 3, 0]
        for j in range(3):
            proj(Pq[j], sAQ, 0, 384, j, T, xbf, 8)
            K.act(bfA[:, qsl[j], 0:T], Pq[j][:, 0:T], AF.Square)
        Pstq = PS[5]
        for j in range(3):
            K.mm(Pstq[:, 0:T], ones[:, :], bfA[:, qsl[j], 0:T], start=(j == 0), stop=(j == 2))
        if KSUB <= 1:
            return
        rstd_from(Pst, T, 1.0 / 256, RMS_EPS)
        if KSUB <= 2:
            return
        for j in range(2):
            K.stt(ckv_f[:, j, 0:T], Pkv[j][:, 0:T], cvc(l, C_KVG + j), st2[:, 0:T], ALU.mult, ALU.mult)
            K.copy("pool", ckv_bf[:, j, 0:T], ckv_f[:, j, 0:T])
        if KSUB <= 3:
            return
        K.tt("dve", tAB[64:96, 0, 0:T], Pkr[64:96, 0:T], cosv, ALU.mult)
        K.tt("dve", tAB[64:96, 1, 0:T], Pkr[96:128, 0:T], sinv, ALU.mult)
        K.tt("pool", kr_f[64:96, 0:T], tAB[64:96, 0, 0:T], tAB[64:96, 1, 0:T], ALU.add)
        if KSUB <= 4:
            return
        for h in range(8):
            K.copy("pool" if h % 2 == 0 else "act", KT[64:96, h, pos0:pos0 + T], kr_f[64:96, 0:T])
        if KSUB <= 5:
            return
        expand_kv(T, pos0, kb0, sKV, ckv_bf)
        rstd_from(Pstq, T, 1.0 / 384, RMS_EPS)
        if KSUB <= 6:
            return
        for sb in range(nsub):
            nk = min(128, T - sb * 128)
            P = PS[5]
            xpose(4 + sb % 2, [ckv_f[:, j, sb * 128:sb * 128 + nk] for j in range(2)], 128, nk, outst[0:nk, sb, :])
            xpose(5 - sb % 2, [kr_f[64:96, sb * 128:sb * 128 + nk]], 32, nk, krst[0:nk, sb, :], kbase=64)
        if KSUB <= 7:
            return
        if kind == "p":
            K.dma("sp", "o_ckv", ockvp[l, s, pos0:pos0 + 512, :].rearrange("(a p) d -> p a d", p=128),
                  outst[:, :, :].ap, in_v=outst[:, :, :])
            K.dma("sp", "o_kr", okrp[l, s, pos0:pos0 + 512, :].rearrange("(a p) d -> p a d", p=128),
                  krst[:, :, :].ap, in_v=krst[:, :, :])
        else:
            K.dma("sp", "o_ckv", ockvs[l, s, :, :], outst[0:TS, 0, :].ap, in_v=outst[0:TS, 0, :])
            K.dma("sp", "o_kr", okrs[l, s, :, :], krst[0:TS, 0, :].ap, in_v=krst[0:TS, 0, :])

        if KSTOP <= 1:
            return
        for j in range(3):
            K.stt(qn_bf[:, j, 0:T], Pq[j][:, 0:T], cvc(l, C_QG + j), st2[:, 0:T], ALU.mult, ALU.mult)
        sUQ = nextw()
        nkeys = pos0 + T
        nkb = (nkeys + 127) // 128
        pti = [0]
        def emit_Q(h):
            hb = h % 2
            Pqh = PS[h % 2]
            for j in range(3):
                K.mm(Pqh[:, 0:T], W(sUQ, j * 1024 + h * 128, 128), qn_bf[:, j, 0:T], start=(j == 0), stop=(j == 2))
            K.copy("act", QT[0:64, hb, 0:T], Pqh[0:64, 0:T])
            K.tt("dve", tAB[64:96, 0, 0:T], Pqh[64:96, 0:T], cosv, ALU.mult)
            K.tt("dve", tAB[64:96, 1, 0:T], Pqh[96:128, 0:T], sinv, ALU.mult)
            K.tt("pool", QT[64:96, hb, 0:T], tAB[64:96, 0, 0:T], tAB[64:96, 1, 0:T], ALU.add)

        emit_Q(0)
        pend = [None]
        for h in range(8):
            hb = h % 2
            Pnum = PS[2] if hb == 0 else PS[6]
            Pden = PS[3] if hb == 0 else PS[7]
            kbl = []
            for kb in range(nkb):
                nk = min(128, nkeys - kb * 128)
                jd = kb - kb0 if kind == "p" else -1
                c0 = 128 * jd if jd > 0 else 0
                kbl.append((kb, nk, jd, c0, 4 + (pti[0] % 2), pti[0] % 3))
                pti[0] += 1

            def S_(idx):
                kb, nk, jd, c0, sbk, p3 = kbl[idx]
                Ps = PS[sbk]
                K.mm(Ps[0:nk, c0:T], KT[0:96, h, kb * 128:kb * 128 + nk], QT[0:96, hb, c0:T])
                K.act(PT[0:nk, p3, c0:T], Ps[0:nk, c0:T], AF.Exp, scale=ATTN_SCALE)
                if jd >= 0:
                    K.act(PT[64:128, p3, c0:c0 + 64], PT[64:128, p3, c0:c0 + 64], AF.Identity, scale=0.0)

            def PV_(idx):
                kb, nk, jd, c0, sbk, p3 = kbl[idx]
                K.mm(Pnum[0:65, c0:T], Vb[0:nk, kb, h, :], PT[0:nk, p3, c0:T],
                     start=(idx == 0), stop=(idx == nkb - 1))

            S_(0)
            if h + 1 < 8:
                emit_Q(h + 1)
            if pend[0] is not None:
                pend[0][0]()
            for idx in range(1, nkb):
                S_(idx)
                PV_(idx - 1)
                if idx == min(3, nkb - 1) and pend[0] is not None:
                    pend[0][1]()
                    pend[0] = None
            PV_(nkb - 1)
            if pend[0] is not None:
                pend[0][1]()
                pend[0] = None

            def normA_(h=h, hb=hb, Pnum=Pnum, Pden=Pden):
                K.copy("act", d32[64:65, 0:T], Pnum[64:65, 0:T])
                K.copy("act", dhl[64:65, 0, 0:T], Pnum[64:65, 0:T])
                K.tt("dve", dhl[64:65, 1, 0:T], d32[64:65, 0:T], dhl[64:65, 0, 0:T], ALU.subtract)

            def normB_(h=h, hb=hb, Pnum=Pnum, Pden=Pden):
                K.mm(Pden[0:64, 0:T], ones[64:65, 0:64], dhl[64:65, 0, 0:T], start=True, stop=False)
                K.mm(Pden[0:64, 0:T], ones[64:65, 0:64], dhl[64:65, 1, 0:T], start=False, stop=True)
                K.add("dve", lambda e, o=recb[0:64, 0:T].ap, i=Pden[0:64, 0:T].ap: e.reciprocal(o, i),
                      [Pden[0:64, 0:T]], [recb[0:64, 0:T]])
                K.tt("dve", oT[hb * 64:hb * 64 + 64, h // 2, 0:T], Pnum[0:64, 0:T], recb[0:64, 0:T], ALU.mult)

            pend[0] = (normA_, normB_)
        pend[0][0]()
        pend[0][1]()
        pend[0] = None

        if KSTOP <= 2:
            return
        sC = nextw()
        Pc = [pb(), pb(), pb(), pb()]
        for cc in range(4):
            proj(Pc[cc], sC, 0, 512, cc, T, xbf, 8)
        for c in range(2):
            K.act(tAB[:, c, 0:T], Pc[2 + c][:, 0:T], AF.Tanh, scale=0.5)
            K.ts("dve", tAB[:, c, 0:T], tAB[:, c, 0:T], 0.5, ALU.mult, 0.5, ALU.add)
            K.tt("dve", upf[:, c, 30:30 + T], tAB[:, c, 0:T], Pc[c][:, 0:T], ALU.mult)
            K.copy("pool", up_bf[:, c, 0:30 + T], upf[:, c, 0:30 + T])
        Pcv = [PS[6], PS[7]]
        for c in range(2):
            sCD = nextw()
            for j in range(31):
                K.mm(Pcv[c][:, 0:T], W(sCD, j * 128, 128), up_bf[:, c, j:j + T], start=(j == 0), stop=(j == 30))
        for c in range(2):
            K.act(cvs[:, c, 0:T], Pcv[c][:, 0:T], AF.Identity, bias=cvc(l, C_CDB + c))
            K.act(bfA[:, c, 0:T], Pcv[c][:, 0:T], AF.Identity, bias=cvc(l, C_CDB + c))
            K.act(bfA[:, 2 + c, 0:T], Pcv[c][:, 0:T], AF.Square, bias=cvc(l, C_CDB + c))
        if KSTOP <= 3:
            return
        sS0 = nextw()
        sS1 = nextw()
        Pg = [pb(), pb(), pb(), pb()]
        for cc in range(4):
            proj(Pg[cc], sS0, 0, 512, cc, T, xbf, 8)
        for c in range(2):
            K.copy("act", gb_sb[:, c, 0:T], Pg[c][:, 0:T])
        Ph = [PS[4], PS[5]]
        for c in range(2):
            proj(Ph[c], sS1, 0, 256, c, T, xbf, 8)
            K.copy("act", h_sb[:, c, 0:T], Ph[c][:, 0:T])
            K.tt("dve", zpf[:, c, 2:2 + T], Pg[2 + c][:, 0:T], h_sb[:, c, 0:T], ALU.mult)
            K.copy("pool", zp_bf[:, c, 0:2 + T], zpf[:, c, 0:2 + T])
        for c in range(2):
            Pz = pb()
            for j in range(3):
                K.mm(Pz[:, 0:T], W(sS1, 2048 + (c * 3 + j) * 128, 128), zp_bf[:, c, j:j + T], start=(j == 0), stop=(j == 2))
            K.tt("dve", scin[:, c, 0:T], Pz[:, 0:T], gb_sb[:, c, 0:T], ALU.mult)

        P1, P2 = PS[4], PS[5]
        for c in range(2):
            K.mm(P1[:, 0:T], ones[:, :], bfA[:, c, 0:T], start=(c == 0), stop=(c == 1))
        for c in range(2):
            K.mm(P2[:, 0:T], ones[:, :], bfA[:, 2 + c, 0:T], start=(c == 0), stop=(c == 1))
        ln_stats(P1, P2, T, 256)
        for c in range(2):
            K.tt("dve", cvs[:, c, 0:T], cvs[:, c, 0:T], st1[:, 0:T], ALU.subtract)
            K.tt("dve", cvs[:, c, 0:T], cvs[:, c, 0:T], st2[:, 0:T], ALU.mult)
            K.act(tAB[:, c, 0:T], cvs[:, c, 0:T], AF.Tanh, bias=cvc(l, C_HCLB + c), scale=cvc(l, C_HCLG + c))
            K.act(cvs[:, c, 0:T], cvs[:, c, 0:T], AF.Identity, bias=cvc(l, C_CLB + c), scale=cvc(l, C_CLG + c))
            K.ts("dve", tAB[:, c, 0:T], tAB[:, c, 0:T], 0.5, ALU.mult, 0.5, ALU.add)
            K.tt("dve", conf_act[:, c, 0:T], cvs[:, c, 0:T], tAB[:, c, 0:T], ALU.mult)

        if kind == "s" or i == 3:
            xpose(4, [upf[:, c, T:T + 30] for c in range(2)], 128, 30, sst[0:30, :])
            K.dma("sp", "o_st", (oconfp if kind == "p" else oconfs)[l, s, :, :], sst[0:30, :].ap, in_v=sst[0:30, :])
            xpose(5, [zpf[:, c, T:T + 2] for c in range(2)], 128, 2, sst[0:2, :])
            K.dma("sp", "o_st", (oscp if kind == "p" else oscs)[l, s, :, :], sst[0:2, :].ap, in_v=sst[0:2, :])
        elif kind == "p":
            for c in range(2):
                K.copy("pool", upf[:, c, 0:30], upf[:, c, T:T + 30])
                K.copy("pool", zpf[:, c, 0:2], zpf[:, c, T:T + 2])

        if KSTOP <= 4:
            return
        for c in range(8):
            sM = nextw()
            mi = c % 2
            for b in range(3):
                Pgt = pb()
                Pbr = pb()
                proj(Pgt, sM, 1024 + b * 1024, 128, 0, T, xbf, 8)
                if b == 0:
                    proj(Pbr, sM, 0, 128, 0, T, oT, 4)
                elif b == 1:
                    proj(Pbr, sM, 512, 128, 0, T, conf_act, 2)
                else:
                    proj(Pbr, sM, 768, 128, 0, T, scin, 2)
                ti2 = (c * 3 + b) % 2
                K.act(t_sb[:, ti2, 0:T], Pgt[:, 0:T], AF.Tanh, bias=cvc(l, C_HBG + b * 8 + c), scale=0.5)
                if b == 0:
                    K.stt(macc[:, mi, 0:T], t_sb[:, ti2, 0:T], 1.0, Pbr[:, 0:T], ALU.add, ALU.mult)
                else:
                    K.stt(mtmp[:, ti2, 0:T], t_sb[:, ti2, 0:T], 1.0, Pbr[:, 0:T], ALU.add, ALU.mult)
                    K.tt("pool", macc[:, mi, 0:T], macc[:, mi, 0:T], mtmp[:, ti2, 0:T], ALU.add)
            K.act(merged_bf[:, c, 0:T], macc[:, mi, 0:T], AF.Identity, scale=0.5)

        if KSTOP <= 5:
            return
        P1, P2 = PS[4], PS[5]

        def stat_mm(c):
            zi = c % 2
            K.mm(P1[:, 0:T], ones[:, :], bfA[:, zi, 0:T], start=(c == 0), stop=(c == 7))
            K.mm(P2[:, 0:T], ones[:, :], bfA[:, 2 + zi, 0:T], start=(c == 0), stop=(c == 7))

        for c in range(8):
            if c % 4 == 0:
                sX = nextw()
            Pm = pb()
            proj(Pm, sX, 0, 512, c % 4, T, merged_bf, 8)
            K.stt(xv(c), xv(c), DN_ALPHA, Pm[:, 0:T], ALU.mult, ALU.add)
            zi = c % 2
            K.copy("act", bfA[:, zi, 0:T], xv(c))
            K.act(bfA[:, 2 + zi, 0:T], xv(c), AF.Square)
            if c > 0:
                stat_mm(c - 1)
        stat_mm(7)
        ln_stats(P1, P2, T, 1024)
        layernorm_apply(xv, T, l, C_L1G, C_L1B, xbf)

        if KSTOP <= 6:
            return
        for a in range(2):
            for j in range(4):
                sF = nextw()
                for cc in range(4):
                    hc = j * 4 + cc
                    Phh = pb()
                    proj(Phh, sF, 0, 512, cc, T, xbf, 8)
                    ri = hc % 2
                    K.act(tAB[:, ri, 0:T], Phh[:, 0:T], AF.Relu, bias=cvc(l, C_BF1 + a * 16 + hc))
                    K.tt(alt(), hbuf[:, hc, 0:T], tAB[:, ri, 0:T], tAB[:, ri, 0:T], ALU.mult)
            if a == 1 and nxt is not None:
                for c in range(8):
                    K.copy(alt(), xbf[:, c, 0:512], xres[:, nxt, c, 0:512])
            for j in range(4):
                sF = nextw()
                for cc in range(2):
                    c = j * 2 + cc
                    Py = pb()
                    proj(Py, sF, 0, 256, cc, T, hbuf, 16)
                    if a == 0:
                        ri = c % 2
                        K.act(tAB[:, ri, 0:T], Py[:, 0:T], AF.Identity, bias=cvc(l, C_BF2 + c))
                        K.stt(xv(c), xv(c), DN_ALPHA, tAB[:, ri, 0:T], ALU.mult, ALU.add)
                    else:
                        K.tt("dve", xv(c), xv(c), Py[:, 0:T], ALU.add)
                        zi = c % 2
                        K.copy("act", bfA[:, zi, 0:T], xv(c))
                        K.act(bfA[:, 2 + zi, 0:T], xv(c), AF.Square)
                        if c > 0:
                            stat_mm(c - 1)
        stat_mm(7)
        ln_stats(P1, P2, T, 1024)
        layernorm_apply(xv, T, l, C_L2G, C_L2B, None)

        if l == nl - 1:
            for g in range(2):
                for sb in range(nsub):
                    nk = min(128, T - sb * 128)
                    xpose((g * 4 + sb) % 4, [xres[:, ti, g * 4 + cc, sb * 128:sb * 128 + nk] for cc in range(4)], 128, nk,
                          xst[0:nk, sb, g * 512:(g + 1) * 512])
            if kind == "p":
                K.dma("sp", "yout", yp[s, pos0:pos0 + 512, :].rearrange("(a p) d -> p a d", p=128),
                      xst[:, :, :].ap, in_v=xst[:, :, :])
            else:
                K.dma("sp", "yout", ys[s, :, :], xst[0:TS, 0, :].ap, in_v=xst[0:TS, 0, :])

    def ingest_cache(s, l):
        slot = wget()
        K.dma("sp", "c_ckv", cst[:, :, :].ap, cckv[l, s, :, :].rearrange("(a p) d -> p a d", p=128), out_v=cst[:, :, :])
        if os.environ.get('KV1', '1') == '1':
            K.memset("dve", kst[:, :, :], 0.0)
        K.dma("sp", "c_kr", kst[:, :, 64:96].ap, ckr[l, s, :, :].rearrange("(a p) d -> p a d", p=128), out_v=kst[:, :, 64:96])
        if KING <= 1:
            return
        cbf = sub("cbf", 10240, [128, 4, 256], BF16)
        kbf = sub("kbf", 12288, [128, 4, 96], BF16)
        for g in range(4):
            K.copy("pool", cbf[:, :, :], cst[:, g * 4:(g + 1) * 4, :])
            K.copy("pool", kbf[:, :, :], kst[:, g * 4:(g + 1) * 4, :])
            if KING <= 3:
                continue
            bj = [(2 * g) % 4, (2 * g + 1) % 4]
            for kk in range(4):
                for j in range(2):
                    K.tr(PSb[bj[j]][:, kk * 128:(kk + 1) * 128], cbf[:, kk, j * 128:(j + 1) * 128], identb[:, :])
            K.copy("act", ckv_bf2[:, 0, 0:512], PSb[bj[0]][:, 0:512])
            K.copy("dve", ckv_bf2[:, 1, 0:512], PSb[bj[1]][:, 0:512])
            if KING <= 4:
                continue
            expand_kv(512, g * 512, g * 4, slot, ckv_bf2)
            if KING <= 5:
                continue
            Pk = PSb[4 + g % 2]
            for kk in range(4):
                K.tr(Pk[0:96, kk * 128:(kk + 1) * 128], kbf[:, kk, :], identb[:, :])
            if os.environ.get('KV2', '0') == '1':
                continue
            ktmp = sub("ktmp", 9216, [128, 512], BF16)
            K.copy("act", ktmp[0:96, :], Pk[0:96, 0:512])
            for h in range(8):
                K.copy("dve" if h % 2 == 0 else "pool", KT[64:96, h, g * 512:(g + 1) * 512], ktmp[64:96, :])
        if KING <= 6:
            return
        K.dma("sp", "st_in", sst[0:30, :].ap, sconf[l, s, :, :], out_v=sst[0:30, :])
        for c in range(2):
            xpose(c, [sst[0:30, c * 128:(c + 1) * 128]], 30, 128, upf[:, c, 0:30])
        K.dma("sp", "st_in", sst[0:2, :].ap, ssc[l, s, :, :], out_v=sst[0:2, :])
        for c in range(2):
            xpose(2 + c, [sst[0:2, c * 128:(c + 1) * 128]], 2, 128, zpf[:, c, 0:2])

    descs = []
    for (kind, s_) in units:
        for l in range(nl):
            if kind == "p":
                for i in range(min(4, KTILES)):
                    descs.append((kind, s_, l, i))
            else:
                descs.append((kind, s_, l, 0))
    pre = False
    for di, (kind, s_, l, i) in enumerate(descs):
        if kind == "p" and i == 0:
            for c in range(2):
                K.memset("pool", upf[:, c, 0:30], 0.0)
                K.memset("pool", zpf[:, c, 0:2], 0.0)
        fbc = None
        if kind == "p" and s_ == 0 and i >= 1 and l + 1 < nl:
            fbc = (l + 2) * NBLK
        nxt = None
        if di + 1 < len(descs):
            nk_, ns_, nl_, ni_ = descs[di + 1]
            if kind == "p" and nk_ == "p" and nl_ > 0 and KSTOP > 7:
                nxt = ni_
        tile(kind, s_, l, i, fbc, nl, pre_xbf=pre, nxt=nxt)
        pre = nxt is not None
    K.emit()
    K.close()
    return nc, K

def _chunkK(M):
    Kd, N = M.shape
    nk = Kd // 128
    return np.ascontiguousarray(M.reshape(nk, 128, N).transpose(1, 0, 2)).reshape(128, nk * N)


def _pack_weights(w_in, w_uq, w_uk, w_uv, w_mla_out, conf_dw_w, w_conf_out, sc_dw_w, w_sc_out, w_mix_out,
                  w_ff1, w_ff2):
    out = np.zeros((NL * NBLK * 128, BW), np.float32)
    swap = np.concatenate([np.arange(16, 32), np.arange(0, 16)])
    eye = np.eye(128, dtype=np.float32)
    for l in range(NL):
        Win = w_in[l]

        def put(b, arr):
            r0 = (l * NBLK + b) * 128
            out[r0:r0 + 128, 0:arr.shape[1]] = arr

        kr = np.zeros((DM, 128), np.float32)
        kr[:, 64:96] = Win[:, 640:672]
        kr[:, 96:128] = Win[:, 640:672][:, swap]
        put(B_AKV, _chunkK(np.concatenate([Win[:, 384:640], kr], axis=1)))
        kv = np.concatenate([w_uk[l].reshape(256, 512), w_uv[l].reshape(256, 512)], axis=1)
        put(B_KV, _chunkK(kv))
        put(B_AQ, _chunkK(Win[:, 0:384]))
        uq = np.zeros((384, 8, 128), np.float32)
        uq[:, :, 0:96] = w_uq[l]
        uq[:, :, 96:128] = w_uq[l][:, :, 64:96][:, :, swap]
        put(B_UQ, _chunkK(uq.reshape(384, 1024)))
        put(B_C, _chunkK(Win[:, 672:1184]))
        for c in range(2):
            d = np.zeros((128, 31, 128), np.float32)
            for j in range(31):
                d[:, j, :] = eye * conf_dw_w[l][j, c * 128:(c + 1) * 128][:, None]
            put(B_CD0 + c, d.reshape(128, 31 * 128))
        put(B_S0, _chunkK(Win[:, 1184:1696]))
        d = np.zeros((128, 6, 128), np.float32)
        for c in range(2):
            for j in range(3):
                d[:, c * 3 + j, :] = eye * sc_dw_w[l][j, c * 128:(c + 1) * 128][:, None]
        put(B_S1, np.concatenate([_chunkK(Win[:, 1696:1952]), d.reshape(128, 768)], axis=1))
        for c in range(8):
            cs = slice(c * 128, (c + 1) * 128)
            parts = [_chunkK(w_mla_out[l][:, cs]), _chunkK(w_conf_out[l][:, cs]), _chunkK(w_sc_out[l][:, cs])]
            for b in range(3):
                parts.append(_chunkK(Win[:, 1952 + b * 1024 + c * 128:1952 + b * 1024 + (c + 1) * 128]))
            put(B_M0 + c, np.concatenate(parts, axis=1))
        for g in range(2):
            put(B_X0 + g, _chunkK(w_mix_out[l][:, g * 512:(g + 1) * 512]))
        for a in range(2):
            for j in range(4):
                put((B_F1A if a == 0 else B_F1B) + j, _chunkK(w_ff1[l][:, a * 2048 + j * 512:a * 2048 + (j + 1) * 512]))
                put((B_F2A if a == 0 else B_F2B) + j, _chunkK(w_ff2[l][a * 2048:(a + 1) * 2048, j * 256:(j + 1) * 256]))
    return out


def _pack_vecs(b_gate, q_norm_g, kv_norm_g, conf_dw_b, conf_ln_g, conf_ln_b, ln1_g, ln1_b, b_ff1, b_ff2, ln2_g, ln2_b):
    cv = np.zeros((128, NL * NCOL), np.float32)

    def cols(v):
        return np.ascontiguousarray(v.reshape(-1, 128).T)

    for l in range(NL):
        o = l * NCOL
        cv[:, o + C_BG:o + C_BG + 24] = cols(b_gate[l].reshape(-1))
        cv[:, o + C_QG:o + C_QG + 3] = cols(q_norm_g[l])
        cv[:, o + C_KVG:o + C_KVG + 2] = cols(kv_norm_g[l])
        cv[:, o + C_CDB:o + C_CDB + 2] = cols(conf_dw_b[l])
        cv[:, o + C_CLG:o + C_CLG + 2] = cols(conf_ln_g[l])
        cv[:, o + C_CLB:o + C_CLB + 2] = cols(conf_ln_b[l])
        cv[:, o + C_L1G:o + C_L1G + 8] = cols(ln1_g[l])
        cv[:, o + C_L1B:o + C_L1B + 8] = cols(ln1_b[l])
        cv[:, o + C_BF1:o + C_BF1 + 32] = cols(b_ff1[l])
        cv[:, o + C_BF2:o + C_BF2 + 8] = cols(b_ff2[l])
        cv[:, o + C_L2G:o + C_L2G + 8] = cols(ln2_g[l])
        cv[:, o + C_L2B:o + C_L2B + 8] = cols(ln2_b[l])
    return cv


def _rope_table():
    half = 16
    inv = (np.float32(10000.0) ** (-np.arange(half, dtype=np.float32) / np.float32(half))).astype(np.float32)
    pos = np.arange(NPOS, dtype=np.float32)
    ang = (pos[None, :] * inv[:, None]).astype(np.float32)
    cos = np.cos(ang).astype(np.float32)
    sin = np.sin(ang).astype(np.float32)
    t = np.zeros((128, NPOS), np.float32)
    t[64:80] = cos
    t[80:96] = cos
    t[96:112] = -sin
    t[112:128] = sin
    return t


_PROG = {}


def _get_prog(nsp=NSP, nss=NSS, nl=NL):
    key = (nsp, nss, nl)
    if key not in _PROG:
        _PROG[key] = build_program(nsp, nss, nl)[0]
    return _PROG[key]


def _make_in_maps(x_prompt, x_sample, cache_ckv, cache_krope, state_conf, state_sc, wf, cvv, ropet):
    f = lambda a: np.ascontiguousarray(np.asarray(a, dtype=np.float32))
    ident = np.eye(128, dtype=np.float32)
    maps = []
    for k in range(8):
        maps.append({
            "xp": f(x_prompt[k * NSP:(k + 1) * NSP]),
            "xs": f(x_sample[k * NSS:(k + 1) * NSS]),
            "cckv": f(cache_ckv[:, k * NSS:(k + 1) * NSS]),
            "ckr": f(cache_krope[:, k * NSS:(k + 1) * NSS]),
            "sconf": f(state_conf[:, k * NSS:(k + 1) * NSS]),
            "ssc": f(state_sc[:, k * NSS:(k + 1) * NSS]),
            "wf32": wf, "cvec": cvv, "rope": ropet, "identd": ident,
        })
    return maps


def kernel(x_prompt, x_sample, cache_ckv, cache_krope, state_conf, state_sc, w_in, b_gate, q_norm_g, w_uq,
           kv_norm_g, w_uk, w_uv, w_mla_out, conf_dw_w, conf_dw_b, conf_ln_g, conf_ln_b, w_conf_out, sc_dw_w,
           w_sc_out, w_mix_out, ln1_g, ln1_b, w_ff1, b_ff1, w_ff2, b_ff2, ln2_g, ln2_b, _cfg=None, _ncores=8):
    A = lambda a: np.asarray(a, dtype=np.float32)
    wf = _pack_weights(A(w_in), A(w_uq), A(w_uk), A(w_uv), A(w_mla_out), A(conf_dw_w), A(w_conf_out), A(sc_dw_w),
                       A(w_sc_out), A(w_mix_out), A(w_ff1), A(w_ff2))
    cvv = _pack_vecs(A(b_gate), A(q_norm_g), A(kv_norm_g), A(conf_dw_b), A(conf_ln_g), A(conf_ln_b), A(ln1_g),
                     A(ln1_b), A(b_ff1), A(b_ff2), A(ln2_g), A(ln2_b))
    maps = _make_in_maps(A(x_prompt), A(x_sample), A(cache_ckv), A(cache_krope), A(state_conf), A(state_sc),
                         wf, cvv, _rope_table())
    nc = _get_prog(*(_cfg or (NSP, NSS, NL)))
    res = run_bass_kernel_spmd(nc, maps[:_ncores], core_ids=list(range(_ncores)))
    R = res.results
    cat0 = lambda n: np.concatenate([np.asarray(r[n]) for r in R], axis=0)
    cat1 = lambda n: np.concatenate([np.asarray(r[n]) for r in R], axis=1)
    return (cat0("yp"), cat0("ys"), cat1("ockvp"), cat1("okrp"), cat1("oconfp"), cat1("oscp"),
            cat1("ockvs"), cat1("okrs"), cat1("oconfs"), cat1("oscs"))
```

```python
import numpy as np
import concourse.bass as bass
import concourse.mybir as mybir
from concourse.bass_utils import run_bass_kernel_spmd

F32 = mybir.dt.float32
BF16 = mybir.dt.bfloat16
AF = mybir.ActivationFunctionType
ALU = mybir.AluOpType
ESZ = {F32: 4, BF16: 2}


class V:
    __slots__ = ("ap", "key", "lo", "hi")

    def __init__(self, ap, key, lo, hi):
        self.ap, self.key, self.lo, self.hi = ap, key, lo, hi


class Buf:
    def __init__(self, ap, key, shape, dtype, off=0):
        self.ap, self.key, self.shape, self.dtype, self.off = ap, key, list(shape), dtype, off
        self.es = ESZ[dtype]
        st = [1] * len(shape)
        for i in range(len(shape) - 2, 0, -1):
            st[i] = st[i + 1] * shape[i + 1]
        self.st = st

    def __getitem__(self, idx):
        if not isinstance(idx, tuple):
            idx = (idx,)
        idx = list(idx) + [slice(None)] * (len(self.shape) - len(idx))
        lo = 0
        hi = 0
        for d in range(1, len(self.shape)):
            i = idx[d]
            if isinstance(i, slice):
                a = 0 if i.start is None else i.start
                b = self.shape[d] if i.stop is None else i.stop
            else:
                a, b = i, i + 1
            assert 0 <= a < b <= self.shape[d], (self.key, idx, self.shape)
            lo += a * self.st[d]
            hi += (b - 1) * self.st[d]
        if getattr(self, "whole", False):
            return V(self.ap[tuple(idx)], self.key, 0, 1 << 30)
        return V(self.ap[tuple(idx)], self.key, self.off + lo * self.es, self.off + (hi + 1) * self.es)


class Op:
    __slots__ = ("eng", "fn", "deps", "signal", "stream", "cnt", "idx")


class Kern:
    ENGS = ("pe", "act", "dve", "pool", "sp")

    def __init__(self, nc):
        self.nc = nc
        self.ops = {e: [] for e in self.ENGS}
        self.rec = {}
        self.streams = {}
        self.ctx = []

    def sbuf(self, name, shape, dtype):
        cm = self.nc.sbuf_tensor(name, list(shape), dtype)
        t = cm.__enter__()
        self.ctx.append(cm)
        b = Buf(t[tuple(slice(None) for _ in shape)], name, shape, dtype)
        b.t = t
        return b

    def psum(self, name, shape, dtype=F32):
        cm = self.nc.psum_tensor(name, list(shape), dtype)
        t = cm.__enter__()
        self.ctx.append(cm)
        b = Buf(t[tuple(slice(None) for _ in shape)], name, shape, dtype)
        b.t = t
        b.whole = True
        return b

    def sub(self, parent, name, off_bytes, shape, dtype):
        n = int(np.prod(shape[1:]))
        es_p = parent.es
        nb = n * ESZ[dtype]
        assert off_bytes % 4 == 0 and off_bytes + nb <= parent.shape[1] * es_p, (name, off_bytes, nb)
        ap = parent.ap[0:shape[0], off_bytes // es_p:(off_bytes + nb) // es_p]
        if dtype != parent.dtype:
            ap = ap.bitcast(dtype)
        if len(shape) == 3:
            ap = ap.rearrange("p (a b) -> p a b", b=shape[2])
        return Buf(ap, parent.key, shape, dtype, off=parent.off + off_bytes)

    def add(self, eng, fn, reads=(), writes=(), stream=None):
        op = Op()
        op.eng, op.fn, op.signal, op.stream = eng, fn, False, stream
        lst = self.ops[eng]
        op.idx = len(lst)
        deps = set()
        for v in reads:
            for r in self.rec.get(v.key, ()):
                if r[2] == "W" and r[0] < v.hi and v.lo < r[1]:
                    deps.add(r[3])
        for v in writes:
            for r in self.rec.get(v.key, ()):
                if r[0] < v.hi and v.lo < r[1]:
                    deps.add(r[3] if r[2] == "W" else ("war",) + r[3])
        if stream is not None:
            self.streams[stream] = self.streams.get(stream, 0) + 1
            op.cnt = self.streams[stream]
            who = ("dma", stream, op.cnt)
        else:
            who = ("eng", eng, op.idx)
        fdeps = set()
        for d in deps:
            war = d[0] == "war"
            if war:
                d = d[1:]
            if d[0] == "dma" and d[1].startswith("cast"):
                d = ("dma", d[1], self.streams[d[1]])
            if d[0] == "eng" and d[1] == eng and stream is None:
                if eng == "pe":
                    continue
            fdeps.add(d)
        op.deps = fdeps
        for v in reads:
            L = self.rec.setdefault(v.key, [])
            for r in L:
                if r[2] == "R" and r[0] == v.lo and r[1] == v.hi and r[3][0] == who[0] and r[3][1] == who[1]:
                    r[3] = who
                    break
            else:
                L.append([v.lo, v.hi, "R", who])
        for v in writes:
            L = self.rec.setdefault(v.key, [])
            L[:] = [r for r in L if not (v.lo <= r[0] and r[1] <= v.hi)]
            L.append([v.lo, v.hi, "W", who])
        lst.append(op)
        return op

    def dma(self, q, stream, out, in_, out_v=None, in_v=None):
        reads = [in_v] if in_v is not None else []
        writes = [out_v] if out_v is not None else []
        return self.add(q, lambda e: e.dma_start(out=out, in_=in_), reads, writes, stream=stream)

    def emit(self):
        nc = self.nc
        for e in self.ENGS:
            for op in self.ops[e]:
                for d in op.deps:
                    if d[0] == "eng":
                        self.ops[d[1]][d[2]].signal = True
        sigcnt = {}
        for e in self.ENGS:
            c = 0
            arr = []
            for op in self.ops[e]:
                if op.signal and op.stream is None:
                    c += 1
                arr.append(c)
            sigcnt[e] = arr
        sems = {}
        for e in self.ENGS:
            cm = nc.semaphore("s_" + e)
            sems[("eng", e)] = cm.__enter__()
            self.ctx.append(cm)
        for s in self.streams:
            cm = nc.semaphore("d_" + s)
            sems[("dma", s)] = cm.__enter__()
            self.ctx.append(cm)
        total_waits = [0]

        def emit_eng(e, eng):
            waited = {}
            for op in self.ops[e]:
                need = {}
                for d in op.deps:
                    if d[0] == "eng":
                        k = ("eng", d[1])
                        val = sigcnt[d[1]][d[2]]
                    else:
                        k = ("dma", d[1])
                        val = 16 * d[2]
                    if val > need.get(k, 0):
                        need[k] = val
                for k, val in need.items():
                    if val > waited.get(k, 0):
                        eng.wait_ge(sems[k], val)
                        waited[k] = val
                        total_waits[0] += 1
                ins = op.fn(eng)
                if op.stream is not None:
                    ins.then_inc(sems[("dma", op.stream)], 16)
                elif op.signal:
                    ins.then_inc(sems[("eng", e)], 1)
            if e == "sp":
                for s, c in self.streams.items():
                    eng.wait_ge(sems[("dma", s)], 16 * c)

        with nc.Block() as block:
            @block.tensor
            def _(eng):
                emit_eng("pe", eng)

            @block.scalar
            def _(eng):
                emit_eng("act", eng)

            @block.vector
            def _(eng):
                emit_eng("dve", eng)

            @block.gpsimd
            def _(eng):
                emit_eng("pool", eng)

            @block.sync
            def _(eng):
                emit_eng("sp", eng)
        self.total_waits = total_waits[0]

    def close(self):
        for cm in reversed(self.ctx):
            cm.__exit__(None, None, None)
        self.ctx = []

    def mm(self, out, lhsT, rhs, start=True, stop=True):
        return self.add("pe", lambda e: e.matmul(out.ap, lhsT.ap, rhs.ap, start=start, stop=stop),
                        [lhsT, rhs], [out])

    def tr(self, out, in_, ident):
        return self.add("pe", lambda e: e.transpose(out.ap, in_.ap, ident.ap), [in_, ident], [out])

    def act(self, out, in_, func, bias=None, scale=None, eng="act"):
        kw = {}
        rd = [in_]
        if bias is not None:
            if isinstance(bias, V):
                kw["bias"] = bias.ap
                rd.append(bias)
            else:
                kw["bias"] = bias
        if scale is not None:
            if isinstance(scale, V):
                kw["scale"] = scale.ap
                rd.append(scale)
            else:
                kw["scale"] = scale
        return self.add("act", lambda e: e.activation(out.ap, in_.ap, func, **kw), rd, [out])

    def tt(self, eng, out, in0, in1, op):
        return self.add(eng, lambda e: e.tensor_tensor(out.ap, in0.ap, in1.ap, op), [in0, in1], [out])

    def ts(self, eng, out, in0, s1, op0, s2=None, op1=None):
        rd = [in0]
        a1 = s1
        a2 = s2
        if isinstance(s1, V):
            rd.append(s1)
            a1 = s1.ap
        if isinstance(s2, V):
            rd.append(s2)
            a2 = s2.ap
        if op1 is None:
            return self.add(eng, lambda e: e.tensor_scalar(out.ap, in0.ap, a1, None, op0), rd, [out])
        return self.add(eng, lambda e: e.tensor_scalar(out.ap, in0.ap, a1, a2, op0, op1), rd, [out])

    def stt(self, out, in0, s, in1, op0, op1):
        rd = [in0, in1]
        a = s
        if isinstance(s, V):
            rd.append(s)
            a = s.ap
        return self.add("dve", lambda e: e.scalar_tensor_tensor(out.ap, in0.ap, a, in1.ap, op0, op1), rd, [out])

    def copy(self, eng, out, in_):
        if eng == "act":
            return self.act(out, in_, AF.Copy)
        return self.add(eng, lambda e: e.tensor_copy(out.ap, in_.ap), [in_], [out])

    def memset(self, eng, out, val):
        return self.add(eng, lambda e: e.memset(out.ap, val), [], [out])

NL = 4
NSP = 4
NSS = 2
SEQ = 2048
TS = 16
PAST = 2048
DM = 1024
NPOS = PAST + TS
NBLK = 35
BW = 4096
NCOL = 135
C_BG, C_QG, C_KVG, C_CDB, C_CLG, C_CLB, C_L1G, C_L1B, C_BF1, C_BF2, C_L2G, C_L2B = (
    0, 24, 27, 29, 31, 33, 35, 43, 51, 83, 91, 99)
C_HBG, C_HCLG, C_HCLB = 107, 131, 133
B_AKV, B_KV, B_AQ, B_UQ, B_C, B_CD0, B_CD1, B_S0, B_S1, B_M0, B_X0, B_X1, B_F1A, B_F2A, B_F1B, B_F2B = (
    0, 1, 2, 3, 4, 5, 6, 7, 8, 9, 17, 18, 19, 23, 27, 31)
BUSED = [3072, 2048, 3072, 3072, 4096, 3968, 3968, 4096, 2816] + [4096] * 26
ATTN_SCALE = 96.0 ** -0.5
DN_ALPHA = 8.0 ** 0.25
LN_EPS = 1e-5
RMS_EPS = 1e-6
NWS = 4


def build_program(nsp=NSP, nss=NSS, nl=NL):
    import os
    KSTOP = int(os.environ.get('KSTOP', '99'))
    KTILES = int(os.environ.get('KTILES', '4'))
    KSUB = int(os.environ.get('KSUB', '99'))
    KING = int(os.environ.get('KING', '99'))
    nc = bass.Bass("TRN2", target_bir_lowering=False)
    K = Kern(nc)

    def din(name, shape, dt=F32):
        return nc.dram_tensor(name, list(shape), dt, kind="ExternalInput").ap()

    def dout(name, shape):
        return nc.dram_tensor(name, list(shape), F32, kind="ExternalOutput").ap()

    xp = din("xp", [NSP, SEQ, DM])
    xs = din("xs", [NSS, TS, DM])
    cckv = din("cckv", [NL, NSS, PAST, 256])
    ckr = din("ckr", [NL, NSS, PAST, 32])
    sconf = din("sconf", [NL, NSS, 30, 256])
    ssc = din("ssc", [NL, NSS, 2, 256])
    wf32 = din("wf32", [NL * NBLK * 128, BW])
    cvec = din("cvec", [128, NL * NCOL])
    rope = din("rope", [128, NPOS])
    identd = din("identd", [128, 128])
    wbf = nc.dram_tensor("wbf", [NL * NBLK * 128, BW], BF16, kind="Internal").ap()
    yp = dout("yp", [NSP, SEQ, DM])
    ys = dout("ys", [NSS, TS, DM])
    ockvp = dout("ockvp", [NL, NSP, SEQ, 256])
    okrp = dout("okrp", [NL, NSP, SEQ, 32])
    oconfp = dout("oconfp", [NL, NSP, 30, 256])
    oscp = dout("oscp", [NL, NSP, 2, 256])
    ockvs = dout("ockvs", [NL, NSS, TS, 256])
    okrs = dout("okrs", [NL, NSS, TS, 32])
    oconfs = dout("oconfs", [NL, NSS, 30, 256])
    oscs = dout("oscs", [NL, NSS, 2, 256])

    xres = K.sbuf("xres", [128, 4, 8, 512], F32)
    KT = K.sbuf("KT", [128, 8, NPOS], BF16)
    Vb = K.sbuf("Vb", [128, 17, 8, 65], BF16)
    wbuf = K.sbuf("wbuf", [128, NWS, BW], BF16)
    xbf = K.sbuf("xbf", [128, 8, 512], BF16)
    upf = K.sbuf("upf", [128, 2, 544], F32)
    zpf = K.sbuf("zpf", [128, 2, 516], F32)
    cv = K.sbuf("cv", [128, NL * NCOL], F32)
    ropeT = K.sbuf("ropeT", [128, 2, 512], F32)
    ident = K.sbuf("ident", [128, 128], F32)
    ones = K.sbuf("ones", [128, 128], BF16)
    mhalf = K.sbuf("mhalf", [128, 512], F32)
    mone = K.sbuf("mone", [128, 512], F32)
    epsc = K.sbuf("epsc", [128, 2], F32)
    arena = K.sbuf("arena", [128, 16384], BF16)

    def sub(name, off, shape, dt):
        return K.sub(arena, name, off, shape, dt)

    st1 = sub("st1", 0, [128, 512], F32)
    st2 = sub("st2", 2048, [128, 512], F32)
    st3 = sub("st3", 4096, [128, 512], F32)
    tAB = sub("tAB", 6144, [128, 2, 512], F32)
    bfA = sub("bfA", 10240, [128, 4, 512], BF16)
    ckv_f = sub("ckv_f", 14336, [128, 2, 512], F32)
    ckv_bf = sub("ckv_bf", 18432, [128, 2, 512], BF16)
    kr_f = sub("kr_f", 20480, [128, 512], F32)
    outst = sub("outst", 22528, [128, 4, 256], F32)
    krst = sub("krst", 26624, [128, 4, 32], F32)
    qn_bf = sub("qn_bf", 10240, [128, 3, 512], BF16)
    QT = sub("QT", 18432, [128, 2, 512], BF16)
    PT = sub("PT", 14336, [128, 3, 512], BF16)
    recb = sub("recb", 22528, [128, 512], F32)
    d32 = sub("d32", 24576, [128, 512], F32)
    dhl = sub("dhl", 26624, [128, 2, 512], BF16)
    oT = sub("oT", 28672, [128, 4, 512], BF16)
    conf_act = sub("conf_act", 26624, [128, 2, 512], BF16)
    scin = sub("scin", 24576, [128, 2, 512], BF16)
    cvs = sub("cvs", 14336, [128, 2, 512], F32)
    up_bf = sub("up_bf", 18432, [128, 2, 544], BF16)
    h_sb = sub("h_sb", 20608, [128, 2, 512], F32)
    zp_bf = sub("zp_bf", 18432, [128, 2, 516], BF16)
    gb_sb = sub("gb_sb", 6144, [128, 2, 512], F32)
    mtmp = sub("mtmp", 0, [128, 2, 512], F32)
    macc = sub("macc", 4096, [128, 2, 512], F32)
    t_sb = sub("t_sb", 8192, [128, 2, 512], F32)
    merged_bf = sub("merged_bf", 16384, [128, 8, 512], BF16)
    hbuf = sub("hbuf", 14336, [128, 16, 512], BF16)
    xst = sub("xst", 14336, [128, 4, 1024], F32)
    cst = sub("cst", 14336, [128, 16, 256], F32)
    kst = sub("kst", 0, [128, 16, 96], F32)
    sst = sub("sst", 8192, [128, 256], F32)

    PS = [K.psum("ps%d" % i, [128, 512]) for i in range(8)]
    psrot = [0]

    def pb(pool=(0, 1, 2, 3)):
        psrot[0] += 1
        return PS[pool[psrot[0] % len(pool)]]

    identb = K.sbuf("identb", [128, 128], BF16)
    PSb = []
    for i in range(8):
        bb = Buf(PS[i].t[:, :].bitcast(BF16), PS[i].key, [128, 1024], BF16)
        bb.whole = True
        PSb.append(bb)

    def planes(v):
        a = v.ap.bitcast(BF16).rearrange("p (n t) -> p n t", t=2)
        return (V(a[:, :, 0], v.key, v.lo, v.hi), V(a[:, :, 1], v.key, v.lo, v.hi))

    def xpose(bank, srcs, Kp, M, dst, kbase=0):
        Pb = PSb[bank]
        off = 0
        idv = identb[kbase:kbase + Kp, kbase:kbase + Kp]
        for src in srcs:
            lo, hi = planes(src)
            K.tr(Pb[0:M, off:off + Kp], lo, idv)
            K.tr(Pb[0:M, 512 + off:512 + off + Kp], hi, idv)
            off += Kp
        dlo, dhi = planes(dst)
        K.copy("dve", dlo, Pb[0:M, 0:off])
        K.copy("dve", dhi, Pb[0:M, 512:512 + off])

    units = []
    for s in range(nsp):
        units.append(("p", s))
    for s in range(nss):
        units.append(("s", s))
    worder = []
    for (kind, s) in units:
        for l in range(nl):
            for i in range(4 if kind == "p" else 1):
                if kind == "s":
                    worder.append((l, B_KV))
                for b in range(NBLK):
                    worder.append((l, b))
    wstate = {"loaded": 0, "n": 0, "cast_next": 0, "owner": [-1] * NWS}
    cast_list = [(l, b) for l in range(nl) for b in range(NBLK)]

    def wreg(l, b):
        r = (l * NBLK + b)
        return V(None, "wbf", r * 16, r * 16 + 16)

    def do_cast(upto):
        while wstate["cast_next"] < min(upto, len(cast_list)):
            l, b = cast_list[wstate["cast_next"]]
            r0 = (l * NBLK + b) * 128
            src = wf32[r0:r0 + 128, :]
            dst = wbf[r0:r0 + 128, :]
            K.add("pool", lambda e, dst=dst, src=src: e.dma_start(out=dst, in_=src), [], [wreg(l, b)],
                  stream="cast%d" % (wstate["cast_next"] % 16))
            wstate["cast_next"] += 1

    def wload(n):
        l, b = worder[n]
        assert l * NBLK + b < wstate["cast_next"], ("cast not recorded", l, b)
        slot = n % NWS
        used = BUSED[b]
        r0 = (l * NBLK + b) * 128
        src = wbf[r0:r0 + 128, 0:used]
        dv = wbuf[:, slot, 0:used]
        wstate["owner"][slot] = n
        K.add("sp", lambda e, o=dv.ap, i=src: e.dma_start(out=o, in_=i), [wreg(l, b)], [dv], stream="w%d" % slot)

    def wget():
        n = wstate["n"]
        while wstate["loaded"] < min(n + NWS - 1, len(worder)):
            wload(wstate["loaded"])
            wstate["loaded"] += 1
        wstate["n"] = n + 1
        return n

    def W(nblk, off, n, p0=0, p1=128):
        slot = nblk % NWS
        assert wstate["owner"][slot] == nblk, ("weight slot recycled", nblk, wstate["owner"])
        return wbuf[p0:p1, slot, off:off + n]

    K.dma("sp", "cvl", cv[:, :].ap, cvec, out_v=cv[:, :])
    K.dma("sp", "idl", ident[:, :].ap, identd, out_v=ident[:, :])
    K.copy("dve", identb[:, :], ident[:, :])
    K.memset("dve", ones[:, :], 1.0)
    K.memset("dve", epsc[:, 0:1], LN_EPS)
    K.memset("dve", epsc[:, 1:2], RMS_EPS)
    K.memset("dve", Vb[:, :, :, 64:65], 1.0)
    K.memset("dve", mhalf[:, :], -0.5)
    K.memset("dve", mone[:, :], -1.0)
    for l in range(NL):
        o = l * NCOL
        K.ts("dve", cv[:, o + C_HBG:o + C_HBG + 24], cv[:, o + C_BG:o + C_BG + 24], 0.5, ALU.mult)
        K.ts("dve", cv[:, o + C_HCLG:o + C_HCLG + 4], cv[:, o + C_CLG:o + C_CLG + 4], 0.5, ALU.mult)

    def cvc(l, off):
        return cv[:, l * NCOL + off:l * NCOL + off + 1]

    do_cast(NBLK)

    rope_n = [0]
    out_eng = [0]

    def alt(engs=("dve", "pool")):
        out_eng[0] += 1
        return engs[out_eng[0] % len(engs)]

    def proj(P, slot, woff, ncols, cc, T, rhsbuf, nk, start=True, stop=True):
        for k in range(nk):
            K.mm(P[:, 0:T], W(slot, woff + k * ncols + cc * 128, 128), rhsbuf[:, k, 0:T],
                 start=(start and k == 0), stop=(stop and k == nk - 1))

    def rstd_from(Pst, T, inv_n, eps):
        K.act(st1[:, 0:T], Pst[:, 0:T], AF.Ln, bias=epsc[:, 1:2], scale=inv_n)
        K.act(st2[:, 0:T], st1[:, 0:T], AF.Exp, scale=-0.5)

    def ln_stats(P1, P2, T, n):
        K.act(st1[:, 0:T], P1[:, 0:T], AF.Identity, scale=1.0 / n)
        K.act(st3[:, 0:T], P1[:, 0:T], AF.Square, scale=1.0 / n)
        K.stt(st3[:, 0:T], P2[:, 0:T], 1.0 / n, st3[:, 0:T], ALU.mult, ALU.subtract)
        K.act(st3[:, 0:T], st3[:, 0:T], AF.Ln, bias=epsc[:, 0:1])
        K.act(st2[:, 0:T], st3[:, 0:T], AF.Exp, scale=-0.5)

    ckv_bf2 = sub("ckv_bf2", 6144, [128, 2, 512], BF16)

    def expand_kv(T, kcol0, kb0, slot, ckv_bf):
        for h in range(8):
            P = pb((0, 1, 2))
            for j in range(2):
                K.mm(P[0:64, 0:T], W(slot, j * 1024 + h * 64, 64), ckv_bf[:, j, 0:T], start=(j == 0), stop=(j == 1))
            if h % 2 == 0:
                K.copy("act", KT[0:64, h, kcol0:kcol0 + T], P[0:64, 0:T])
            else:
                K.copy("dve", KT[0:64, h, kcol0:kcol0 + T], P[0:64, 0:T])
        nsub = (T + 127) // 128
        for sb in range(nsub):
            nk = min(128, T - sb * 128)
            P = pb((0, 1, 2))
            for j in range(2):
                K.mm(P[0:nk, 0:512], ckv_bf[:, j, sb * 128:sb * 128 + nk], W(slot, j * 1024 + 512, 512),
                     start=(j == 0), stop=(j == 1))
            pv = P[0:nk, 0:512]
            pv3 = V(pv.ap.rearrange("p (h d) -> p h d", d=64), pv.key, pv.lo, pv.hi)
            K.copy("act" if sb % 2 == 0 else "dve", Vb[0:nk, kb0 + sb, :, 0:64], pv3)

    def layernorm_apply(xv, T, l, cg, cb, xbf_out):
        for c in range(8):
            K.tt("dve", xv(c), xv(c), st1[:, 0:T], ALU.subtract)
            K.tt("pool", xv(c), xv(c), st2[:, 0:T], ALU.mult)
            if xbf_out is not None:
                K.act(xbf_out[:, c, 0:T], xv(c), AF.Identity, bias=cvc(l, cb + c), scale=cvc(l, cg + c))
            K.act(xv(c), xv(c), AF.Identity, bias=cvc(l, cb + c), scale=cvc(l, cg + c))

    def tile(kind, s, l, i, first_block_cast, nl, pre_xbf=False, nxt=None):
        T = 512 if kind == "p" else TS
        pos0 = i * 512 if kind == "p" else PAST
        kb0 = pos0 // 128
        nsub = (T + 127) // 128
        ti = i if kind == "p" else 0

        def xv(c):
            return xres[:, ti, c, 0:T]

        bcount = [0]
        if kind == "s":
            ingest_cache(s, l)

        def nextw():
            if first_block_cast is not None and bcount[0] % 2 == 0:
                do_cast(wstate["cast_next"] + 1 if wstate["cast_next"] < first_block_cast else 0)
            bcount[0] += 1
            return wget()

        if l == 0:
            if kind == "p":
                src = xp[s, pos0:pos0 + 512, :].rearrange("(a p) d -> p a d", p=128)
                K.dma("sp", "xin", xst[:, :, :].ap, src, out_v=xst[:, :, :])
            else:
                K.dma("sp", "xin", xst[0:TS, 0, :].ap, xs[s, :, :], out_v=xst[0:TS, 0, :])
            nk0 = min(128, T)
            for c in range(8):
                xpose(c % 4, [xst[0:nk0, sb, c * 128:(c + 1) * 128] for sb in range(nsub)], nk0, 128, xv(c))
        if not pre_xbf:
            for c in range(8):
                K.copy(alt(), xbf[:, c, 0:T], xv(c))
        rs = rope_n[0] % 2
        rope_n[0] += 1
        K.dma("sp", "rope%d" % rs, ropeT[64:128, rs, 0:T].ap, rope[64:128, pos0:pos0 + T], out_v=ropeT[64:128, rs, 0:T])
        cosv = ropeT[64:96, rs, 0:T]
        sinv = ropeT[96:128, rs, 0:T]

        if KSTOP <= 0:
            return
        sAKV = nextw()
        Pkv = [PS[0], PS[1]]
        proj(Pkv[0], sAKV, 0, 384, 0, T, xbf, 8)
        proj(Pkv[1], sAKV, 0, 384, 1, T, xbf, 8)
        Pkr = PS[2]
        proj(Pkr, sAKV, 0, 384, 2, T, xbf, 8)
        for j in range(2):
            K.act(bfA[:, j, 0:T], Pkv[j][:, 0:T], AF.Square)
        Pst = PS[4]
        for j in range(2):
            K.mm(Pst[:, 0:T], ones[:, :], bfA[:, j, 0:T], start=(j == 0), stop=(j == 1))
        if KSUB <= 1:
            return
        rstd_from(Pst, T, 1.0 / 256, RMS_EPS)
        if KSUB <= 2:
            return
        for j in range(2):
            K.stt(ckv_f[:, j, 0:T], Pkv[j][:, 0:T], cvc(l, C_KVG + j), st2[:, 0:T], ALU.mult, ALU.mult)
            K.copy("pool", ckv_bf[:, j, 0:T], ckv_f[:, j, 0:T])
        if KSUB <= 3:
            return
        K.tt("dve", tAB[64:96, 0, 0:T], Pkr[64:96, 0:T], cosv, ALU.mult)
        K.tt("dve", tAB[64:96, 1, 0:T], Pkr[96:128, 0:T], sinv, ALU.mult)
        K.tt("pool", kr_f[64:96, 0:T], tAB[64:96, 0, 0:T], tAB[64:96, 1, 0:T], ALU.add)
        if KSUB <= 4:
            return
        for h in range(8):
            K.copy("pool" if h % 2 == 0 else "act", KT[64:96, h, pos0:pos0 + T], kr_f[64:96, 0:T])
        if KSUB <= 5:
            return
        sKV = nextw()
        sAQ = nextw()
        Pq = [PS[3], PS[6], PS[7]]
        qsl = [2, 3, 0]
        for j in range(3):
            proj(Pq[j], sAQ, 0, 384, j, T, xbf, 8)
            K.act(bfA[:, qsl[j], 0:T], Pq[j][:, 0:T], AF.Square)
        Pstq = PS[5]
        for j in range(3):
            K.mm(Pstq[:, 0:T], ones[:, :], bfA[:, qsl[j], 0:T], start=(j == 0), stop=(j == 2))
        expand_kv(T, pos0, kb0, sKV, ckv_bf)
        rstd_from(Pstq, T, 1.0 / 384, RMS_EPS)
        for j in range(3):
            K.stt(qn_bf[:, j, 0:T], Pq[j][:, 0:T], cvc(l, C_QG + j), st2[:, 0:T], ALU.mult, ALU.mult)
        sUQ = nextw()
        def emit_Q(h):
            hb = h % 2
            Pqh = PS[h % 2]
            for j in range(3):
                K.mm(Pqh[:, 0:T], W(sUQ, j * 1024 + h * 128, 128), qn_bf[:, j, 0:T], start=(j == 0), stop=(j == 2))
            K.copy("act", QT[0:64, hb, 0:T], Pqh[0:64, 0:T])
            K.tt("dve", tAB[64:96, 0, 0:T], Pqh[64:96, 0:T], cosv, ALU.mult)
            K.tt("dve", tAB[64:96, 1, 0:T], Pqh[96:128, 0:T], sinv, ALU.mult)
            K.tt("pool", QT[64:96, hb, 0:T], tAB[64:96, 0, 0:T], tAB[64:96, 1, 0:T], ALU.add)

        emit_Q(0)
        if KSUB <= 6:
            return
        for sb in range(nsub):
            nk = min(128, T - sb * 128)
            P = PS[5]
            xpose(4 + sb % 2, [ckv_f[:, j, sb * 128:sb * 128 + nk] for j in range(2)], 128, nk, outst[0:nk, sb, :])
            xpose(5 - sb % 2, [kr_f[64:96, sb * 128:sb * 128 + nk]], 32, nk, krst[0:nk, sb, :], kbase=64)
        if KSUB <= 7:
            return
        if kind == "p":
            K.dma("sp", "o_ckv", ockvp[l, s, pos0:pos0 + 512, :].rearrange("(a p) d -> p a d", p=128),
                  outst[:, :, :].ap, in_v=outst[:, :, :])
            K.dma("sp", "o_kr", okrp[l, s, pos0:pos0 + 512, :].rearrange("(a p) d -> p a d", p=128),
                  krst[:, :, :].ap, in_v=krst[:, :, :])
        else:
            K.dma("sp", "o_ckv", ockvs[l, s, :, :], outst[0:TS, 0, :].ap, in_v=outst[0:TS, 0, :])
            K.dma("sp", "o_kr", okrs[l, s, :, :], krst[0:TS, 0, :].ap, in_v=krst[0:TS, 0, :])

        if KSTOP <= 1:
            return
        nkeys = pos0 + T
        nkb = (nkeys + 127) // 128
        pti = [0]
        pend = [None]
        for h in range(8):
            hb = h % 2
            Pnum = PS[2] if hb == 0 else PS[6]
            Pden = PS[3] if hb == 0 else PS[7]
            kbl = []
            for kb in range(nkb):
                nk = min(128, nkeys - kb * 128)
                jd = kb - kb0 if kind == "p" else -1
                c0 = 128 * jd if jd > 0 else 0
                kbl.append((kb, nk, jd, c0, 4 + (pti[0] % 2), pti[0] % 3))
                pti[0] += 1

            def S_(idx):
                kb, nk, jd, c0, sbk, p3 = kbl[idx]
                Ps = PS[sbk]
                K.mm(Ps[0:nk, c0:T], KT[0:96, h, kb * 128:kb * 128 + nk], QT[0:96, hb, c0:T])
                K.act(PT[0:nk, p3, c0:T], Ps[0:nk, c0:T], AF.Exp, scale=ATTN_SCALE)
                if jd >= 0:
                    K.act(PT[64:128, p3, c0:c0 + 64], PT[64:128, p3, c0:c0 + 64], AF.Identity, scale=0.0)

            def PV_(idx):
                kb, nk, jd, c0, sbk, p3 = kbl[idx]
                K.mm(Pnum[0:65, c0:T], Vb[0:nk, kb, h, :], PT[0:nk, p3, c0:T],
                     start=(idx == 0), stop=(idx == nkb - 1))

            S_(0)
            if h + 1 < 8:
                emit_Q(h + 1)
            if pend[0] is not None:
                pend[0][0]()
            for idx in range(1, nkb):
                S_(idx)
                PV_(idx - 1)
                if idx == min(3, nkb - 1) and pend[0] is not None:
                    pend[0][1]()
                    pend[0] = None
            PV_(nkb - 1)
            if pend[0] is not None:
                pend[0][1]()
                pend[0] = None

            def normA_(h=h, hb=hb, Pnum=Pnum, Pden=Pden):
                K.copy("act", d32[64:65, 0:T], Pnum[64:65, 0:T])
                K.copy("act", dhl[64:65, 0, 0:T], Pnum[64:65, 0:T])
                K.tt("dve", dhl[64:65, 1, 0:T], d32[64:65, 0:T], dhl[64:65, 0, 0:T], ALU.subtract)

            def normB_(h=h, hb=hb, Pnum=Pnum, Pden=Pden):
                K.mm(Pden[0:64, 0:T], ones[64:65, 0:64], dhl[64:65, 0, 0:T], start=True, stop=False)
                K.mm(Pden[0:64, 0:T], ones[64:65, 0:64], dhl[64:65, 1, 0:T], start=False, stop=True)
                K.add("dve", lambda e, o=recb[0:64, 0:T].ap, i=Pden[0:64, 0:T].ap: e.reciprocal(o, i),
                      [Pden[0:64, 0:T]], [recb[0:64, 0:T]])
                K.tt("dve", oT[hb * 64:hb * 64 + 64, h // 2, 0:T], Pnum[0:64, 0:T], recb[0:64, 0:T], ALU.mult)

            pend[0] = (normA_, normB_)
        pend[0][0]()
        pend[0][1]()
        pend[0] = None

        if KSTOP <= 2:
            return
        sC = nextw()
        Pc = [pb(), pb(), pb(), pb()]
        for cc in range(4):
            proj(Pc[cc], sC, 0, 512, cc, T, xbf, 8)
        for c in range(2):
            K.act(tAB[:, c, 0:T], Pc[2 + c][:, 0:T], AF.Tanh, scale=0.5)
            K.ts("dve", tAB[:, c, 0:T], tAB[:, c, 0:T], 0.5, ALU.mult, 0.5, ALU.add)
            K.tt("dve", upf[:, c, 30:30 + T], tAB[:, c, 0:T], Pc[c][:, 0:T], ALU.mult)
            K.copy("pool", up_bf[:, c, 0:30 + T], upf[:, c, 0:30 + T])
        Pcv = [PS[6], PS[7]]
        for c in range(2):
            sCD = nextw()
            for j in range(31):
                K.mm(Pcv[c][:, 0:T], W(sCD, j * 128, 128), up_bf[:, c, j:j + T], start=(j == 0), stop=(j == 30))
        for c in range(2):
            K.act(cvs[:, c, 0:T], Pcv[c][:, 0:T], AF.Identity, bias=cvc(l, C_CDB + c))
            K.act(bfA[:, c, 0:T], Pcv[c][:, 0:T], AF.Identity, bias=cvc(l, C_CDB + c))
            K.act(bfA[:, 2 + c, 0:T], Pcv[c][:, 0:T], AF.Square, bias=cvc(l, C_CDB + c))
        if KSTOP <= 3:
            return
        sS0 = nextw()
        sS1 = nextw()
        Pg = [pb(), pb(), pb(), pb()]
        for cc in range(4):
            proj(Pg[cc], sS0, 0, 512, cc, T, xbf, 8)
        for c in range(2):
            K.copy("act", gb_sb[:, c, 0:T], Pg[c][:, 0:T])
        Ph = [PS[4], PS[5]]
        for c in range(2):
            proj(Ph[c], sS1, 0, 256, c, T, xbf, 8)
            K.copy("act", h_sb[:, c, 0:T], Ph[c][:, 0:T])
            K.tt("dve", zpf[:, c, 2:2 + T], Pg[2 + c][:, 0:T], h_sb[:, c, 0:T], ALU.mult)
            K.copy("pool", zp_bf[:, c, 0:2 + T], zpf[:, c, 0:2 + T])
        for c in range(2):
            Pz = pb()
            for j in range(3):
                K.mm(Pz[:, 0:T], W(sS1, 2048 + (c * 3 + j) * 128, 128), zp_bf[:, c, j:j + T], start=(j == 0), stop=(j == 2))
            K.tt("dve", scin[:, c, 0:T], Pz[:, 0:T], gb_sb[:, c, 0:T], ALU.mult)

        P1, P2 = PS[4], PS[5]
        for c in range(2):
            K.mm(P1[:, 0:T], ones[:, :], bfA[:, c, 0:T], start=(c == 0), stop=(c == 1))
        for c in range(2):
            K.mm(P2[:, 0:T], ones[:, :], bfA[:, 2 + c, 0:T], start=(c == 0), stop=(c == 1))
        ln_stats(P1, P2, T, 256)
        for c in range(2):
            K.tt("dve", cvs[:, c, 0:T], cvs[:, c, 0:T], st1[:, 0:T], ALU.subtract)
            K.tt("dve", cvs[:, c, 0:T], cvs[:, c, 0:T], st2[:, 0:T], ALU.mult)
            K.act(tAB[:, c, 0:T], cvs[:, c, 0:T], AF.Tanh, bias=cvc(l, C_HCLB + c), scale=cvc(l, C_HCLG + c))
            K.act(cvs[:, c, 0:T], cvs[:, c, 0:T], AF.Identity, bias=cvc(l, C_CLB + c), scale=cvc(l, C_CLG + c))
            K.ts("dve", tAB[:, c, 0:T], tAB[:, c, 0:T], 0.5, ALU.mult, 0.5, ALU.add)
            K.tt("dve", conf_act[:, c, 0:T], cvs[:, c, 0:T], tAB[:, c, 0:T], ALU.mult)

        if kind == "s" or i == 3:
            xpose(4, [upf[:, c, T:T + 30] for c in range(2)], 128, 30, sst[0:30, :])
            K.dma("sp", "o_st", (oconfp if kind == "p" else oconfs)[l, s, :, :], sst[0:30, :].ap, in_v=sst[0:30, :])
            xpose(5, [zpf[:, c, T:T + 2] for c in range(2)], 128, 2, sst[0:2, :])
            K.dma("sp", "o_st", (oscp if kind == "p" else oscs)[l, s, :, :], sst[0:2, :].ap, in_v=sst[0:2, :])
        elif kind == "p":
            for c in range(2):
                K.copy("pool", upf[:, c, 0:30], upf[:, c, T:T + 30])
                K.copy("pool", zpf[:, c, 0:2], zpf[:, c, T:T + 2])

        if KSTOP <= 4:
            return
        for c in range(8):
            sM = nextw()
            mi = c % 2
            for b in range(3):
                Pgt = pb()
                Pbr = pb()
                proj(Pgt, sM, 1024 + b * 1024, 128, 0, T, xbf, 8)
                if b == 0:
                    proj(Pbr, sM, 0, 128, 0, T, oT, 4)
                elif b == 1:
                    proj(Pbr, sM, 512, 128, 0, T, conf_act, 2)
                else:
                    proj(Pbr, sM, 768, 128, 0, T, scin, 2)
                ti2 = (c * 3 + b) % 2
                K.act(t_sb[:, ti2, 0:T], Pgt[:, 0:T], AF.Tanh, bias=cvc(l, C_HBG + b * 8 + c), scale=0.5)
                if b == 0:
                    K.stt(macc[:, mi, 0:T], t_sb[:, ti2, 0:T], 1.0, Pbr[:, 0:T], ALU.add, ALU.mult)
                else:
                    K.stt(mtmp[:, ti2, 0:T], t_sb[:, ti2, 0:T], 1.0, Pbr[:, 0:T], ALU.add, ALU.mult)
                    K.tt("pool", macc[:, mi, 0:T], macc[:, mi, 0:T], mtmp[:, ti2, 0:T], ALU.add)
            K.act(merged_bf[:, c, 0:T], macc[:, mi, 0:T], AF.Identity, scale=0.5)

        if KSTOP <= 5:
            return
        P1, P2 = PS[4], PS[5]

        def stat_mm(c):
            zi = c % 2
            K.mm(P1[:, 0:T], ones[:, :], bfA[:, zi, 0:T], start=(c == 0), stop=(c == 7))
            K.mm(P2[:, 0:T], ones[:, :], bfA[:, 2 + zi, 0:T], start=(c == 0), stop=(c == 7))

        for c in range(8):
            if c % 4 == 0:
                sX = nextw()
            Pm = pb()
            proj(Pm, sX, 0, 512, c % 4, T, merged_bf, 8)
            K.stt(xv(c), xv(c), DN_ALPHA, Pm[:, 0:T], ALU.mult, ALU.add)
            zi = c % 2
            K.copy("act", bfA[:, zi, 0:T], xv(c))
            K.act(bfA[:, 2 + zi, 0:T], xv(c), AF.Square)
            if c > 0:
                stat_mm(c - 1)
        stat_mm(7)
        ln_stats(P1, P2, T, 1024)
        layernorm_apply(xv, T, l, C_L1G, C_L1B, xbf)

        if KSTOP <= 6:
            return
        for a in range(2):
            for j in range(4):
                sF = nextw()
                for cc in range(4):
                    hc = j * 4 + cc
                    Phh = pb()
                    proj(Phh, sF, 0, 512, cc, T, xbf, 8)
                    ri = hc % 2
                    K.act(tAB[:, ri, 0:T], Phh[:, 0:T], AF.Relu, bias=cvc(l, C_BF1 + a * 16 + hc))
                    K.tt(alt(), hbuf[:, hc, 0:T], tAB[:, ri, 0:T], tAB[:, ri, 0:T], ALU.mult)
            if a == 1 and nxt is not None:
                for c in range(8):
                    K.copy(alt(), xbf[:, c, 0:512], xres[:, nxt, c, 0:512])
            for j in range(4):
                sF = nextw()
                for cc in range(2):
                    c = j * 2 + cc
                    Py = pb()
                    proj(Py, sF, 0, 256, cc, T, hbuf, 16)
                    if a == 0:
                        ri = c % 2
                        K.act(tAB[:, ri, 0:T], Py[:, 0:T], AF.Identity, bias=cvc(l, C_BF2 + c))
                        K.stt(xv(c), xv(c), DN_ALPHA, tAB[:, ri, 0:T], ALU.mult, ALU.add)
                    else:
                        K.tt("dve", xv(c), xv(c), Py[:, 0:T], ALU.add)
                        zi = c % 2
                        K.copy("act", bfA[:, zi, 0:T], xv(c))
                        K.act(bfA[:, 2 + zi, 0:T], xv(c), AF.Square)
                        if c > 0:
                            stat_mm(c - 1)
        stat_mm(7)
        ln_stats(P1, P2, T, 1024)
        layernorm_apply(xv, T, l, C_L2G, C_L2B, None)

        if l == nl - 1:
            for g in range(2):
                for sb in range(nsub):
                    nk = min(128, T - sb * 128)
                    xpose((g * 4 + sb) % 4, [xres[:, ti, g * 4 + cc, sb * 128:sb * 128 + nk] for cc in range(4)], 128, nk,
                          xst[0:nk, sb, g * 512:(g + 1) * 512])
            if kind == "p":
                K.dma("sp", "yout", yp[s, pos0:pos0 + 512, :].rearrange("(a p) d -> p a d", p=128),
                      xst[:, :, :].ap, in_v=xst[:, :, :])
            else:
                K.dma("sp", "yout", ys[s, :, :], xst[0:TS, 0, :].ap, in_v=xst[0:TS, 0, :])

    def ingest_cache(s, l):
        slot = wget()
        K.dma("sp", "c_ckv", cst[:, :, :].ap, cckv[l, s, :, :].rearrange("(a p) d -> p a d", p=128), out_v=cst[:, :, :])
        if os.environ.get('KV1', '1') == '1':
            K.memset("dve", kst[:, :, :], 0.0)
        K.dma("sp", "c_kr", kst[:, :, 64:96].ap, ckr[l, s, :, :].rearrange("(a p) d -> p a d", p=128), out_v=kst[:, :, 64:96])
        if KING <= 1:
            return
        cbf = sub("cbf", 10240, [128, 4, 256], BF16)
        kbf = sub("kbf", 12288, [128, 4, 96], BF16)
        for g in range(4):
            K.copy("pool", cbf[:, :, :], cst[:, g * 4:(g + 1) * 4, :])
            K.copy("pool", kbf[:, :, :], kst[:, g * 4:(g + 1) * 4, :])
            if KING <= 3:
                continue
            bj = [(2 * g) % 4, (2 * g + 1) % 4]
            for kk in range(4):
                for j in range(2):
                    K.tr(PSb[bj[j]][:, kk * 128:(kk + 1) * 128], cbf[:, kk, j * 128:(j + 1) * 128], identb[:, :])
            K.copy("act", ckv_bf2[:, 0, 0:512], PSb[bj[0]][:, 0:512])
            K.copy("dve", ckv_bf2[:, 1, 0:512], PSb[bj[1]][:, 0:512])
            if KING <= 4:
                continue
            expand_kv(512, g * 512, g * 4, slot, ckv_bf2)
            if KING <= 5:
                continue
            Pk = PSb[4 + g % 2]
            for kk in range(4):
                K.tr(Pk[0:96, kk * 128:(kk + 1) * 128], kbf[:, kk, :], identb[:, :])
            if os.environ.get('KV2', '0') == '1':
                continue
            ktmp = sub("ktmp", 9216, [128, 512], BF16)
            K.copy("act", ktmp[0:96, :], Pk[0:96, 0:512])
            for h in range(8):
                K.copy("dve" if h % 2 == 0 else "pool", KT[64:96, h, g * 512:(g + 1) * 512], ktmp[64:96, :])
        if KING <= 6:
            return
        K.dma("sp", "st_in", sst[0:30, :].ap, sconf[l, s, :, :], out_v=sst[0:30, :])
        for c in range(2):
            xpose(c, [sst[0:30, c * 128:(c + 1) * 128]], 30, 128, upf[:, c, 0:30])
        K.dma("sp", "st_in", sst[0:2, :].ap, ssc[l, s, :, :], out_v=sst[0:2, :])
        for c in range(2):
            xpose(2 + c, [sst[0:2, c * 128:(c + 1) * 128]], 2, 128, zpf[:, c, 0:2])

    descs = []
    for (kind, s_) in units:
        for l in range(nl):
            if kind == "p":
                for i in range(min(4, KTILES)):
                    descs.append((kind, s_, l, i))
            else:
                descs.append((kind, s_, l, 0))
    pre = False
    for di, (kind, s_, l, i) in enumerate(descs):
        if kind == "p" and i == 0:
            for c in range(2):
                K.memset("pool", upf[:, c, 0:30], 0.0)
                K.memset("pool", zpf[:, c, 0:2], 0.0)
        fbc = None
        if kind == "p" and s_ == 0 and i >= 1 and l + 1 < nl:
            fbc = (l + 2) * NBLK
        nxt = None
        if di + 1 < len(descs):
            nk_, ns_, nl_, ni_ = descs[di + 1]
            if kind == "p" and nk_ == "p" and nl_ > 0 and KSTOP > 7:
                nxt = ni_
        tile(kind, s_, l, i, fbc, nl, pre_xbf=pre, nxt=nxt)
        pre = nxt is not None
    K.emit()
    K.close()
    return nc, K

def _chunkK(M):
    Kd, N = M.shape
    nk = Kd // 128
    return np.ascontiguousarray(M.reshape(nk, 128, N).transpose(1, 0, 2)).reshape(128, nk * N)


def _pack_weights(w_in, w_uq, w_uk, w_uv, w_mla_out, conf_dw_w, w_conf_out, sc_dw_w, w_sc_out, w_mix_out,
                  w_ff1, w_ff2):
    out = np.zeros((NL * NBLK * 128, BW), np.float32)
    swap = np.concatenate([np.arange(16, 32), np.arange(0, 16)])
    eye = np.eye(128, dtype=np.float32)
    for l in range(NL):
        Win = w_in[l]

        def put(b, arr):
            r0 = (l * NBLK + b) * 128
            out[r0:r0 + 128, 0:arr.shape[1]] = arr

        kr = np.zeros((DM, 128), np.float32)
        kr[:, 64:96] = Win[:, 640:672]
        kr[:, 96:128] = Win[:, 640:672][:, swap]
        put(B_AKV, _chunkK(np.concatenate([Win[:, 384:640], kr], axis=1)))
        kv = np.concatenate([w_uk[l].reshape(256, 512), w_uv[l].reshape(256, 512)], axis=1)
        put(B_KV, _chunkK(kv))
        put(B_AQ, _chunkK(Win[:, 0:384]))
        uq = np.zeros((384, 8, 128), np.float32)
        uq[:, :, 0:96] = w_uq[l]
        uq[:, :, 96:128] = w_uq[l][:, :, 64:96][:, :, swap]
        put(B_UQ, _chunkK(uq.reshape(384, 1024)))
        put(B_C, _chunkK(Win[:, 672:1184]))
        for c in range(2):
            d = np.zeros((128, 31, 128), np.float32)
            for j in range(31):
                d[:, j, :] = eye * conf_dw_w[l][j, c * 128:(c + 1) * 128][:, None]
            put(B_CD0 + c, d.reshape(128, 31 * 128))
        put(B_S0, _chunkK(Win[:, 1184:1696]))
        d = np.zeros((128, 6, 128), np.float32)
        for c in range(2):
            for j in range(3):
                d[:, c * 3 + j, :] = eye * sc_dw_w[l][j, c * 128:(c + 1) * 128][:, None]
        put(B_S1, np.concatenate([_chunkK(Win[:, 1696:1952]), d.reshape(128, 768)], axis=1))
        for c in range(8):
            cs = slice(c * 128, (c + 1) * 128)
            parts = [_chunkK(w_mla_out[l][:, cs]), _chunkK(w_conf_out[l][:, cs]), _chunkK(w_sc_out[l][:, cs])]
            for b in range(3):
                parts.append(_chunkK(Win[:, 1952 + b * 1024 + c * 128:1952 + b * 1024 + (c + 1) * 128]))
            put(B_M0 + c, np.concatenate(parts, axis=1))
        for g in range(2):
            put(B_X0 + g, _chunkK(w_mix_out[l][:, g * 512:(g + 1) * 512]))
        for a in range(2):
            for j in range(4):
                put((B_F1A if a == 0 else B_F1B) + j, _chunkK(w_ff1[l][:, a * 2048 + j * 512:a * 2048 + (j + 1) * 512]))
                put((B_F2A if a == 0 else B_F2B) + j, _chunkK(w_ff2[l][a * 2048:(a + 1) * 2048, j * 256:(j + 1) * 256]))
    return out


def _pack_vecs(b_gate, q_norm_g, kv_norm_g, conf_dw_b, conf_ln_g, conf_ln_b, ln1_g, ln1_b, b_ff1, b_ff2, ln2_g, ln2_b):
    cv = np.zeros((128, NL * NCOL), np.float32)

    def cols(v):
        return np.ascontiguousarray(v.reshape(-1, 128).T)

    for l in range(NL):
        o = l * NCOL
        cv[:, o + C_BG:o + C_BG + 24] = cols(b_gate[l].reshape(-1))
        cv[:, o + C_QG:o + C_QG + 3] = cols(q_norm_g[l])
        cv[:, o + C_KVG:o + C_KVG + 2] = cols(kv_norm_g[l])
        cv[:, o + C_CDB:o + C_CDB + 2] = cols(conf_dw_b[l])
        cv[:, o + C_CLG:o + C_CLG + 2] = cols(conf_ln_g[l])
        cv[:, o + C_CLB:o + C_CLB + 2] = cols(conf_ln_b[l])
        cv[:, o + C_L1G:o + C_L1G + 8] = cols(ln1_g[l])
        cv[:, o + C_L1B:o + C_L1B + 8] = cols(ln1_b[l])
        cv[:, o + C_BF1:o + C_BF1 + 32] = cols(b_ff1[l])
        cv[:, o + C_BF2:o + C_BF2 + 8] = cols(b_ff2[l])
        cv[:, o + C_L2G:o + C_L2G + 8] = cols(ln2_g[l])
        cv[:, o + C_L2B:o + C_L2B + 8] = cols(ln2_b[l])
    return cv


def _rope_table():
    half = 16
    inv = (np.float32(10000.0) ** (-np.arange(half, dtype=np.float32) / np.float32(half))).astype(np.float32)
    pos = np.arange(NPOS, dtype=np.float32)
    ang = (pos[None, :] * inv[:, None]).astype(np.float32)
    cos = np.cos(ang).astype(np.float32)
    sin = np.sin(ang).astype(np.float32)
    t = np.zeros((128, NPOS), np.float32)
    t[64:80] = cos
    t[80:96] = cos
    t[96:112] = -sin
    t[112:128] = sin
    return t


_PROG = {}


def _get_prog(nsp=NSP, nss=NSS, nl=NL):
    key = (nsp, nss, nl)
    if key not in _PROG:
        _PROG[key] = build_program(nsp, nss, nl)[0]
    return _PROG[key]


def _make_in_maps(x_prompt, x_sample, cache_ckv, cache_krope, state_conf, state_sc, wf, cvv, ropet):
    f = lambda a: np.ascontiguousarray(np.asarray(a, dtype=np.float32))
    ident = np.eye(128, dtype=np.float32)
    maps = []
    for k in range(8):
        maps.append({
            "xp": f(x_prompt[k * NSP:(k + 1) * NSP]),
            "xs": f(x_sample[k * NSS:(k + 1) * NSS]),
            "cckv": f(cache_ckv[:, k * NSS:(k + 1) * NSS]),
            "ckr": f(cache_krope[:, k * NSS:(k + 1) * NSS]),
            "sconf": f(state_conf[:, k * NSS:(k + 1) * NSS]),
            "ssc": f(state_sc[:, k * NSS:(k + 1) * NSS]),
            "wf32": wf, "cvec": cvv, "rope": ropet, "identd": ident,
        })
    return maps


def kernel(x_prompt, x_sample, cache_ckv, cache_krope, state_conf, state_sc, w_in, b_gate, q_norm_g, w_uq,
           kv_norm_g, w_uk, w_uv, w_mla_out, conf_dw_w, conf_dw_b, conf_ln_g, conf_ln_b, w_conf_out, sc_dw_w,
           w_sc_out, w_mix_out, ln1_g, ln1_b, w_ff1, b_ff1, w_ff2, b_ff2, ln2_g, ln2_b, _cfg=None, _ncores=8):
    A = lambda a: np.asarray(a, dtype=np.float32)
    wf = _pack_weights(A(w_in), A(w_uq), A(w_uk), A(w_uv), A(w_mla_out), A(conf_dw_w), A(w_conf_out), A(sc_dw_w),
                       A(w_sc_out), A(w_mix_out), A(w_ff1), A(w_ff2))
    cvv = _pack_vecs(A(b_gate), A(q_norm_g), A(kv_norm_g), A(conf_dw_b), A(conf_ln_g), A(conf_ln_b), A(ln1_g),
                     A(ln1_b), A(b_ff1), A(b_ff2), A(ln2_g), A(ln2_b))
    maps = _make_in_maps(A(x_prompt), A(x_sample), A(cache_ckv), A(cache_krope), A(state_conf), A(state_sc),
                         wf, cvv, _rope_table())
    nc = _get_prog(*(_cfg or (NSP, NSS, NL)))
    res = run_bass_kernel_spmd(nc, maps[:_ncores], core_ids=list(range(_ncores)))
    R = res.results
    cat0 = lambda n: np.concatenate([np.asarray(r[n]) for r in R], axis=0)
    cat1 = lambda n: np.concatenate([np.asarray(r[n]) for r in R], axis=1)
    return (cat0("yp"), cat0("ys"), cat1("ockvp"), cat1("okrp"), cat1("oconfp"), cat1("oscp"),
            cat1("ockvs"), cat1("okrs"), cat1("oconfs"), cat1("oscs"))
```
